# Optimizing a Trainium2 kernel written in Bass

```python
import jax, jax.numpy as jnp
from jax import lax
import numpy as np


D_MODEL = 1024
BATCH = 4
SEQ = 4096
DEPTH = 2

CHUNK = 64
N_BRANCH = 3
BRANCH_WIDTH = D_MODEL // 2
POOL_WINDOWS = (2, 4, 8, 16)
N_POOL_GROUPS = len(POOL_WINDOWS)
POOL_GROUP = BRANCH_WIDTH // N_POOL_GROUPS
CONV_K = 3
SB_HEAD_DIM = 64
SB_HEADS = BRANCH_WIDTH // SB_HEAD_DIM
Q_BLOCK = 128
RMS_EPS = 1e-6
IN_SIZES = (BRANCH_WIDTH,) * 10 + (N_BRANCH * D_MODEL,)
N_IN = sum(IN_SIZES)

kernel_name = "hybrid_pool_conv_stickbreak_block"


def _split_points():
    return [int(p) for p in np.cumsum(IN_SIZES)[:-1]]


def rms_norm(x, g):
    xf = x.astype(jnp.float32)
    y = xf * lax.rsqrt(jnp.mean(xf * xf, axis=-1, keepdims=True) + RMS_EPS)
    return (y * g.astype(jnp.float32)).astype(x.dtype)


def pool_mixer(v, w_group, scale):
    b, s, _ = v.shape
    vg = v.astype(jnp.float32).reshape(b, s, N_POOL_GROUPS, POOL_GROUP)
    csum = jnp.cumsum(vg, axis=1)
    pos = jnp.arange(s)
    outs = []
    for gi, w in enumerate(POOL_WINDOWS):
        c = csum[:, :, gi]
        lag = jnp.pad(c[:, :s - w], ((0, 0), (w, 0), (0, 0)))
        cnt = jnp.minimum(pos + 1, w).astype(jnp.float32)[None, :, None]
        outs.append((c - lag) / cnt - vg[:, :, gi])
    pooled = jnp.stack(outs, axis=2).astype(v.dtype)
    mixed = jnp.einsum('bsgc,gcd->bsgd', pooled, w_group)
    return mixed.reshape(b, s, BRANCH_WIDTH) * scale


def conv_mixer(xc, gate_b, gate_c, w, bias):
    z = gate_c * xc
    y = lax.conv_general_dilated(
        z, w[:, None, :].astype(z.dtype), window_strides=(1,), padding=[(CONV_K - 1, 0)],
        dimension_numbers=('NWC', 'WIO', 'NWC'), feature_group_count=BRANCH_WIDTH)
    return gate_b * (y + bias)


def stick_breaking_attention(q, k, v):
    b, s, _ = q.shape
    nblk = s // Q_BLOCK
    qb = q.reshape(b, nblk, Q_BLOCK, SB_HEADS, SB_HEAD_DIM).transpose(1, 0, 2, 3, 4)
    kf = k.reshape(b, s, SB_HEADS, SB_HEAD_DIM).astype(jnp.float32)
    vf = v.reshape(b, s, SB_HEADS, SB_HEAD_DIM).astype(jnp.float32)
    key_pos = jnp.arange(s)
    scale = SB_HEAD_DIM ** -0.5

    def block(args):
        qi, i = args
        logits = jnp.einsum('bqhd,bkhd->bhqk', qi.astype(jnp.float32), kf) * scale
        q_pos = i * Q_BLOCK + jnp.arange(Q_BLOCK)
        mask = key_pos[None, :] < q_pos[:, None]
        log_keep = jnp.where(mask, jax.nn.log_sigmoid(-logits), 0.0)
        later = lax.cumsum(log_keep, axis=3, reverse=True) - log_keep
        weights = jnp.where(mask, jnp.exp(jax.nn.log_sigmoid(logits) + later), 0.0)
        return jnp.einsum('bhqk,bkhd->bqhd', weights, vf)

    out = lax.map(block, (qb, jnp.arange(nblk)))
    return out.transpose(1, 0, 2, 3, 4).reshape(b, s, BRANCH_WIDTH).astype(q.dtype)


def hybrid_layer(x, g_pre, w_in, pool_w, pool_scale, conv_w, conv_b, w_branch, w_out, g_post):
    b, s, _ = x.shape
    h = rms_norm(x, g_pre)
    u = jnp.einsum('bsd,dn->bsn', h, w_in)
    (pool_v, pool_g, conv_x, conv_gb, conv_gc, conv_g,
     sb_q, sb_k, sb_v, sb_g, merge) = jnp.split(u, _split_points(), axis=-1)
    y_pool = pool_mixer(pool_v, pool_w, pool_scale) * jax.nn.silu(pool_g)
    y_conv = conv_mixer(conv_x, conv_gb, conv_gc, conv_w, conv_b) * jax.nn.silu(conv_g)
    y_sb = stick_breaking_attention(sb_q, sb_k, sb_v) * jax.nn.silu(sb_g)
    branches = jnp.stack([y_pool, y_conv, y_sb], axis=2)
    proj = jnp.einsum('bsnw,nwd->bsnd', branches, w_branch)
    gates = jax.nn.sigmoid(merge.reshape(b, s, N_BRANCH, D_MODEL))
    merged = jnp.sum(gates * proj, axis=2)
    out = jnp.einsum('bsd,de->bse', merged, w_out)
    return x + rms_norm(out, g_post)


def setup_inputs(seed: int = 0) -> dict:
    key = jax.random.key(seed)
    ks = jax.random.split(key, 10)
    f32 = jnp.float32
    x = jax.random.normal(ks[0], (BATCH, SEQ, D_MODEL), f32)
    pre_norm_g = 1.0 + 0.05 * jax.random.normal(ks[1], (DEPTH, D_MODEL), f32)
    w_in = jax.random.normal(ks[2], (DEPTH, D_MODEL, N_IN), f32) * D_MODEL ** -0.5
    pool_w = jax.random.normal(ks[3], (DEPTH, N_POOL_GROUPS, POOL_GROUP, POOL_GROUP), f32) * POOL_GROUP ** -0.5
    pool_scale = 1.0 + 0.1 * jax.random.normal(ks[4], (DEPTH, BRANCH_WIDTH), f32)
    conv_w = jax.random.normal(ks[5], (DEPTH, CONV_K, BRANCH_WIDTH), f32) * CONV_K ** -0.5
    conv_b = 0.01 * jax.random.normal(ks[6], (DEPTH, BRANCH_WIDTH), f32)
    w_branch = jax.random.normal(ks[7], (DEPTH, N_BRANCH, BRANCH_WIDTH, D_MODEL), f32) * BRANCH_WIDTH ** -0.5
    w_out = jax.random.normal(ks[8], (DEPTH, D_MODEL, D_MODEL), f32) * D_MODEL ** -0.5
    post_norm_g = 1.0 + 0.05 * jax.random.normal(ks[9], (DEPTH, D_MODEL), f32)
    return {"x": x, "pre_norm_g": pre_norm_g, "w_in": w_in, "pool_w": pool_w,
            "pool_scale": pool_scale, "conv_w": conv_w, "conv_b": conv_b,
            "w_branch": w_branch, "w_out": w_out, "post_norm_g": post_norm_g}


def reference(x, pre_norm_g, w_in, pool_w, pool_scale, conv_w, conv_b, w_branch, w_out, post_norm_g):
    for l in range(DEPTH):
        x = hybrid_layer(x, pre_norm_g[l], w_in[l], pool_w[l], pool_scale[l], conv_w[l],
                         conv_b[l], w_branch[l], w_out[l], post_norm_g[l])
    return x
```

```python
import numpy as np
import concourse.bass as bass
import concourse.mybir as mybir
from contextlib import ExitStack

F32 = mybir.dt.float32
BF16 = mybir.dt.bfloat16
AF = mybir.ActivationFunctionType
ALU = mybir.AluOpType
AX = mybir.AxisListType

CENG = ["pe", "act", "dve", "pool"]
NDMASEM = 24


import types as _types


def _freeze(fn):
    if fn is None or fn.__closure__ is None:
        return fn
    cells = []
    for c in fn.__closure__:
        try:
            cells.append(_types.CellType(c.cell_contents))
        except ValueError:
            cells.append(c)
    return _types.FunctionType(fn.__code__, fn.__globals__, fn.__name__, fn.__defaults__, tuple(cells))


class Buf:
    __slots__ = ("name", "w", "r")

    def __init__(self, name=""):
        self.name = name
        self.w = None
        self.r = []


class Op:
    __slots__ = ("eng", "fn", "deps", "kind", "sig", "semi", "semval", "idx")

    def __init__(self, eng, fn, kind):
        self.eng = eng
        self.fn = fn
        self.deps = []
        self.kind = kind
        self.sig = False
        self.semi = None
        self.semval = None
        self.idx = None


class Prog:
    def __init__(self, nc):
        self.nc = nc
        self.ops = []
        self.per_eng = {e: [] for e in CENG + ["sp"]}
        self.dma_rr = 0
        self.dma_rr_k = {}
        self.dma_last = [None] * NDMASEM
        self.dma_cnt = [0] * NDMASEM
        self.out_dmas = []

    def _track(self, op, reads, writes):
        deps = []
        for b in reads:
            if b.w is not None:
                deps.append(b.w)
        for b in writes:
            if b.w is not None:
                deps.append(b.w)
            deps.extend(b.r)
        for b in reads:
            b.r.append(op)
        for b in writes:
            b.w = op
            b.r = []
        seen = set()
        for d in deps:
            if d is op or id(d) in seen:
                continue
            if d.kind == "c" and op.kind == "c" and d.eng == "pe" and op.eng == "pe":
                continue
            seen.add(id(d))
            op.deps.append(d)
            d.sig = True

    def op(self, eng, fn, reads=(), writes=()):
        o = Op(eng, _freeze(fn), "c")
        self._track(o, reads, writes)
        self.ops.append(o)
        self.per_eng[eng].append(o)
        return o

    def dma(self, fn, reads=(), writes=(), q="sp", is_out=False, inc=16):
        o = Op(q, _freeze(fn), "d")
        o.idx = inc
        if inc == 1:
            lo, n = 22, 2
        elif q == "pool":
            lo, n = 14, 8
        else:
            lo, n = 0, 14
        k = self.dma_rr_k.get(lo, 0)
        self.dma_rr_k[lo] = (k + 1) % n
        s = lo + k
        prev = self.dma_last[s]
        self._track(o, reads, writes)
        if prev is not None and prev not in o.deps:
            o.deps.append(prev)
        self.dma_cnt[s] += inc
        o.semi = s
        o.semval = self.dma_cnt[s]
        o.sig = True
        self.dma_last[s] = o
        self.ops.append(o)
        self.per_eng[q].append(o)
        if is_out:
            self.out_dmas.append(o)
        return o

    def barrier(self):
        lasts = []
        for e in CENG:
            if self.per_eng[e]:
                for o in reversed(self.per_eng[e]):
                    if o.kind == "c" and o.fn is not None:
                        lasts.append(o)
                        break
        for s in range(NDMASEM):
            if self.dma_last[s] is not None and self.dma_last[s].idx != 1:
                lasts.append(self.dma_last[s])
        for e in CENG + ["sp"]:
            o = Op(e, None, "c")
            for d in lasts:
                if d.kind == "c" and d.eng == e and e == "pe":
                    continue
                o.deps.append(d)
                d.sig = True
            self.ops.append(o)
            self.per_eng[e].append(o)

    def emit(self):
        nc = self.nc
        with ExitStack() as st:
            esem = {e: st.enter_context(nc.semaphore("s_" + e)) for e in CENG}
            dsem = [st.enter_context(nc.semaphore("d_%d" % i)) for i in range(NDMASEM)]
            block = st.enter_context(nc.Block())
            cnt = {e: 0 for e in CENG}
            for o in self.ops:
                if o.kind == "c" and o.fn is not None:
                    if o.sig:
                        cnt[o.eng] += 1
                        o.semval = cnt[o.eng]
            fin = Op("sp", None, "c")
            fin.deps = list(self.out_dmas)
            self.per_eng["sp"].append(fin)

            def run(engname, eng):
                waited = {}
                for o in self.per_eng[engname]:
                    need = {}
                    for d in o.deps:
                        if d.kind == "c":
                            if d.fn is None:
                                continue
                            key = ("e", d.eng)
                        else:
                            key = ("d", d.semi)
                        if d.semval > need.get(key, 0):
                            need[key] = d.semval
                    for key, v in need.items():
                        if waited.get(key, 0) >= v:
                            continue
                        waited[key] = v
                        sem = esem[key[1]] if key[0] == "e" else dsem[key[1]]
                        eng.wait_ge(sem, v)
                    if o.fn is None:
                        continue
                    inst = o.fn(eng)
                    if o.kind == "d":
                        inst.then_inc(dsem[o.semi], o.idx)
                    elif o.sig:
                        inst.then_inc(esem[o.eng], 1)

            @block.tensor
            def _(e):
                run("pe", e)

            @block.scalar
            def _(e):
                run("act", e)

            @block.vector
            def _(e):
                run("dve", e)

            @block.gpsimd
            def _(e):
                run("pool", e)

            @block.sync
            def _(e):
                run("sp", e)


S = 4096
D = 1024
TOK = 2048
NTB = 16
EPS = 1e-6


def _alloc(nc, st):
    sb = lambda n, s, d: st.enter_context(nc.sbuf_tensor(n, s, d))
    ps = lambda n, s, d: st.enter_context(nc.psum_tensor(n, s, d))
    return sb, ps


def emit_norm_block(P, nc, T, B, x_src_fn, tb_idx, hT, hT_buf, col0, gpre, slot):
    xt, xtB = T["xt"][slot], B["xt"][slot]
    hb, hbB = T["hb"][slot], B["hb"][slot]
    st, stB = T["st"], B["st"]
    P.dma(lambda e: e.dma_start(out=xt[:], in_=x_src_fn()), writes=[xtB])
    c = tb_idx
    P.op("act", lambda e: e.activation(out=T["junk"][:], in_=xt[:], func=AF.Square, accum_out=st[:, 3 * c:3 * c + 1]),
         reads=[xtB], writes=[B["junk"], stB])
    P.op("act", lambda e: e.activation(out=st[:, 3 * c + 1:3 * c + 2], in_=st[:, 3 * c:3 * c + 1], func=AF.Ln, bias=EPS, scale=1.0 / D),
         reads=[stB], writes=[stB])
    P.op("act", lambda e: e.activation(out=st[:, 3 * c + 2:3 * c + 3], in_=st[:, 3 * c + 1:3 * c + 2], func=AF.Exp, scale=-0.5),
         reads=[stB], writes=[stB])
    P.op("dve", lambda e: e.tensor_scalar_mul(out=hb[:], in0=xt[:], scalar1=st[:, 3 * c + 2:3 * c + 3]),
         reads=[xtB, stB], writes=[hbB])
    tp, tpB = T["tp"], B["tp"]
    for dc in range(8):
        P.op("pe", lambda e, dc=dc: e.transpose(tp[:, dc, :], hb[:, dc * 128:(dc + 1) * 128], T["ident"][:]),
             reads=[hbB, B["const"]], writes=[tpB])
    for dc in range(8):
        eng = "dve" if dc % 2 == 0 else "pool"
        if eng == "pool":
            eng = "dve"
        P.op(eng, lambda e, dc=dc: e.tensor_scalar_mul(out=hT[:, dc, col0:col0 + 128], in0=tp[:, dc, :], scalar1=gpre[:, dc:dc + 1]),
             reads=[tpB, B["const"]], writes=[hT_buf])


def build_A(dbg=False):
    nc = bass.Bass("TRN2", target_bir_lowering=False)
    dt_in = lambda n, s: nc.dram_tensor(n, s, F32, kind="ExternalInput").ap()
    xf = dt_in("xf", [S, D])
    gpre_d = dt_in("gpre_d", [128, 8])
    wq_d = dt_in("wq", [D, 1024])
    cst_d = dt_in("cst_d", [128, 4 * 128])
    ysb_d = nc.dram_tensor("ysb", [2, 128, S], F32, kind="ExternalOutput").ap()
    P = Prog(nc)
    with ExitStack() as st_:
        sb, ps = _alloc(nc, st_)
        T, B = {}, {}
        T["xt"] = [sb("xt%d" % i, [128, D], F32) for i in range(2)]
        B["xt"] = [Buf() for i in range(2)]
        T["hb"] = [sb("hb%d" % i, [128, D], BF16) for i in range(2)]
        B["hb"] = [Buf() for i in range(2)]
        T["junk"] = sb("junk", [128, D], BF16); B["junk"] = Buf()
        T["st"] = sb("st", [128, 3 * 32], F32); B["st"] = Buf()
        cst = sb("cst", [128, 512], BF16); B["const"] = Buf()
        T["ident"] = cst[:, 0:128]
        negtri = cst[:, 128:256]; negones = cst[:, 256:384]; cmask = cst[:, 384:512]
        gpre = sb("gpre", [128, 8], F32)
        hT = sb("hT", [128, 8, S], BF16); hTB = Buf()
        wqb = sb("wqb", [128, 8, 1024], BF16); wqB = Buf()
        qT = sb("qT", [128, 2, S], BF16); kT = sb("kT", [128, 2, S], BF16)
        sgT = sb("sgT", [128, 2, S], BF16); vt = sb("vt", [128, 32, 256], BF16)
        qB, kB, sgB, vB = Buf(), Buf(), Buf(), Buf()
        ysb = sb("ysbT", [128, 2, S], BF16); ysbB = Buf()
        ee = [sb("ee%d" % i, [128, 512], F32) for i in range(2)]; eeB = [Buf() for _ in range(2)]
        spp = [sb("spp%d" % i, [128, 512], BF16) for i in range(3)]; sppB = [Buf() for _ in range(3)]
        AT = [sb("AT%d" % i, [128, 512], BF16) for i in range(2)]; ATB = [Buf() for _ in range(2)]
        Sf = [sb("Sf%d" % i, [128, 512], F32) for i in range(2)]; SfB = [Buf() for _ in range(2)]
        Sb = [[sb("Sb%d_%d" % (i, j), [128, 512], BF16) for j in range(2)] for i in range(2)]
        SbB = [[Buf() for j in range(2)] for i in range(2)]
        T["tp"] = ps("tp", [128, 8, 128], BF16); B["tp"] = Buf()
        Z1 = [ps("Z1%d" % i, [128, 512], F32) for i in range(2)]; Z1B = [Buf() for _ in range(2)]
        Z2 = [ps("Z2%d" % i, [128, 512], F32) for i in range(2)]; Z2B = [Buf() for _ in range(2)]
        OO = [ps("OO%d" % i, [128, 512], F32) for i in range(2)]; OOB = [Buf() for _ in range(2)]

        P.dma(lambda e: e.dma_start(out=cst[:], in_=cst_d), writes=[B["const"]], q="pool")
        P.dma(lambda e: e.dma_start(out=gpre[:], in_=gpre_d), writes=[B["const"]])
        P.dma(lambda e: e.dma_start(out=wqb[:], in_=wq_d.rearrange("(kc p) n -> p kc n", p=128)), writes=[wqB], q="pool")
        for tb in range(32):
            emit_norm_block(P, nc, T, B, (lambda tb=tb: xf[tb * 128:(tb + 1) * 128, :]), tb, hT, hTB, tb * 128, gpre, tb % 2)
        k = 0
        for cc in range(6):
            for tc in range(8):
                zi = k % 2; k += 1
                for kc in range(8):
                    P.op("pe", lambda e, zi=zi, kc=kc, cc=cc, tc=tc: e.matmul(Z1[zi][:], wqb[:, kc, cc * 128:(cc + 1) * 128], hT[:, kc, tc * 512:(tc + 1) * 512], start=(kc == 0), stop=(kc == 7)),
                         reads=[wqB, hTB], writes=[Z1B[zi]])
                sl = slice(tc * 512, (tc + 1) * 512)
                if cc < 2:
                    P.op("dve", lambda e, zi=zi, cc=cc, sl=sl: e.tensor_scalar_mul(out=qT[:, cc, sl], in0=Z1[zi][:], scalar1=0.125), reads=[Z1B[zi]], writes=[qB])
                elif cc < 4:
                    P.op("dve", lambda e, zi=zi, cc=cc, sl=sl: e.tensor_copy(out=kT[:, cc - 2, sl], in_=Z1[zi][:]), reads=[Z1B[zi]], writes=[kB])
                else:
                    P.op("act", lambda e, zi=zi, cc=cc, sl=sl: e.activation(out=sgT[:, cc - 4, sl], in_=Z1[zi][:], func=AF.Silu), reads=[Z1B[zi]], writes=[sgB])
        for tb in range(32):
            zi = k % 2; k += 1
            for kc in range(8):
                P.op("pe", lambda e, zi=zi, kc=kc, tb=tb: e.matmul(Z1[zi][:, 0:256], hT[:, kc, tb * 128:(tb + 1) * 128], wqb[:, kc, 768:1024], start=(kc == 0), stop=(kc == 7)),
                     reads=[wqB, hTB], writes=[Z1B[zi]])
            P.op("dve", lambda e, zi=zi, tb=tb: e.tensor_copy(out=vt[:, tb, :], in_=Z1[zi][:, 0:256]), reads=[Z1B[zi]], writes=[vB])

        tiles = []
        for hp in range(2):
            for G in range(8):
                for kb in range(4 * G + 3, -1, -1):
                    for hh in range(2):
                        tiles.append((hp, G, kb, hh))
        nS = {}

        def stage1(t, tl):
            hp, G, kb, hh = tl
            r = kb - 4 * G if kb >= 4 * G else 0
            c0 = 128 * r
            zi, ei, si = t % 2, t % 2, t % 3
            ps_ = slice(hh * 64, (hh + 1) * 64)
            P.op("pe", lambda e: e.matmul(Z1[zi][:, c0:512], kT[ps_, hp, kb * 128:(kb + 1) * 128], qT[ps_, hp, G * 512 + c0:(G + 1) * 512], start=True, stop=True),
                 reads=[kB, qB], writes=[Z1B[zi]])
            P.op("act", lambda e: e.activation(out=ee[ei][:, c0:512], in_=Z1[zi][:, c0:512], func=AF.Exp), reads=[Z1B[zi]], writes=[eeB[ei]])
            P.op("act", lambda e: e.activation(out=spp[si][:, c0:512], in_=ee[ei][:, c0:512], func=AF.Ln, bias=1.0, scale=1.0), reads=[eeB[ei]], writes=[sppB[si]])
            if kb >= 4 * G:
                P.op("pool", lambda e: e.tensor_tensor(out=spp[si][:, c0:c0 + 128], in0=spp[si][:, c0:c0 + 128], in1=cmask, op=ALU.mult),
                     reads=[sppB[si], B["const"]], writes=[sppB[si]])

        def stage2(t, tl):
            hp, G, kb, hh = tl
            diag = kb >= 4 * G
            r = kb - 4 * G if diag else 0
            c0 = 128 * r
            zi, si, ai = t % 2, t % 3, t % 2
            s = hh
            top = (kb == 4 * G + 3)
            n = nS.get((hp, G, hh), 0)
            nS[(hp, G, hh)] = n + 1
            ps_ = slice(hh * 64, (hh + 1) * 64)
            if top:
                P.op("dve", lambda e: e.memset(Sf[s][:], 0.0), writes=[SfB[s]])
            P.op("pe", lambda e: e.matmul(Z2[zi][:, c0:512], kT[ps_, hp, kb * 128:(kb + 1) * 128], qT[ps_, hp, G * 512 + c0:(G + 1) * 512], start=True, stop=False),
                 reads=[kB, qB], writes=[Z2B[zi]])
            P.op("pe", lambda e: e.matmul(Z2[zi][:, c0:512], negtri, spp[si][:, c0:512], start=False, stop=top),
                 reads=[sppB[si], B["const"]], writes=[Z2B[zi]])
            if not top:
                pj = (n - 1) % 2
                P.op("pe", lambda e: e.matmul(Z2[zi][:, c0:512], negones, Sb[s][pj][:, c0:512], start=False, stop=True),
                     reads=[SbB[s][pj], B["const"]], writes=[Z2B[zi]])
            if kb > 0:
                P.op("dve", lambda e: e.tensor_tensor(out=Sf[s][:, c0:512], in0=Sf[s][:, c0:512], in1=spp[si][:, c0:512], op=ALU.add),
                     reads=[SfB[s], sppB[si]], writes=[SfB[s]])
                P.op("dve", lambda e: e.tensor_copy(out=Sb[s][n % 2][:], in_=Sf[s][:]), reads=[SfB[s]], writes=[SbB[s][n % 2]])
            P.op("act", lambda e: e.activation(out=AT[ai][:, c0:512], in_=Z2[zi][:, c0:512], func=AF.Exp), reads=[Z2B[zi]], writes=[ATB[ai]])
            if diag:
                P.op("pool", lambda e: e.tensor_tensor(out=AT[ai][:, c0:c0 + 128], in0=AT[ai][:, c0:c0 + 128], in1=cmask, op=ALU.mult),
                     reads=[ATB[ai], B["const"]], writes=[ATB[ai]])
            vsl = vt[:, kb, hp * 128:(hp + 1) * 128]
            last = (kb == 0)
            o0 = 0 if top else c0
            if top:
                P.op("pool", lambda e: e.memset(AT[ai][:, 0:c0], 0.0), writes=[ATB[ai]])
            P.op("pe", lambda e: e.matmul(OO[s][:, o0:512], vsl, AT[ai][:, o0:512], start=top, stop=last),
                 reads=[vB, ATB[ai]], writes=[OOB[s]])
            if last:
                gs = slice(G * 512, (G + 1) * 512)
                P.op("dve", lambda e: e.tensor_tensor(out=ysb[ps_, hp, gs], in0=OO[s][ps_, :], in1=sgT[ps_, hp, gs], op=ALU.mult),
                     reads=[OOB[s], sgB], writes=[ysbB])

        if dbg:
            tiles = [tl for tl in tiles if tl[0] == 0 and tl[1] == 0]
        stage1(0, tiles[0])
        for t in range(len(tiles)):
            if t + 1 < len(tiles):
                stage1(t + 1, tiles[t + 1])
            stage2(t, tiles[t])
        if dbg:
            dbg_d = nc.dram_tensor("dbg", [128, 5 * 512], F32, kind="ExternalOutput").ap()
            dsb = sb("dsb", [128, 2 * 512], F32); dsbB = Buf()
            P.op("act", lambda e: e.activation(out=dsb[:, 0:512], in_=OO[0][:], func=AF.Copy), reads=[OOB[0]], writes=[dsbB])
            P.op("act", lambda e: e.activation(out=dsb[:, 512:1024], in_=OO[1][:], func=AF.Copy), reads=[OOB[1]], writes=[dsbB])
            P.dma(lambda e: e.dma_start(out=dbg_d[:, 0:512], in_=Sf[0][:]), reads=[SfB[0]], is_out=True)
            P.dma(lambda e: e.dma_start(out=dbg_d[:, 512:1024], in_=Sf[1][:]), reads=[SfB[1]], is_out=True)
            P.dma(lambda e: e.dma_start(out=dbg_d[:, 1024:2048], in_=dsb[:]), reads=[dsbB], is_out=True)
            P.dma(lambda e: e.dma_start(out=dbg_d[:, 2048:2560], in_=Sb[0][0][:]), reads=[SbB[0][0]], is_out=True, q="pool")
        for hp in range(2):
            P.dma(lambda e, hp=hp: e.dma_start(out=ysb_d[hp], in_=ysb[:, hp, :]), reads=[ysbB], is_out=True, q="pool")
        P.emit()
    return nc


def build_B():
    nc = bass.Bass("TRN2", target_bir_lowering=False)
    dt_in = lambda n, s: nc.dram_tensor(n, s, F32, kind="ExternalInput").ap()
    xo = dt_in("xo", [TOK + 128, D])
    gpre_d = dt_in("gpre_d", [128, 8])
    cst_d = dt_in("cst_d", [128, 128])
    pm_d = dt_in("pm_d", [128, 12 * 128])
    wpool_d = dt_in("wpool", [D, 1024])
    poolw_d = dt_in("poolw_d", [128, 4 * 128])
    vec_d = dt_in("vec_d", [128, 32])
    wconv_d = dt_in("wconv", [4, D, 512])
    wmg_d = dt_in("wmg", [8, D, 384])
    wbr_d = dt_in("wbr", [8, 512, 3 * 384 // 3 * 1])
    wout_d = dt_in("wout", [D, D])
    gpost_d = dt_in("gpost_d", [128, D])
    ysb_d = dt_in("ysbm", [4, 128, TOK])
    xn_d = nc.dram_tensor("xn", [TOK, D], F32, kind="ExternalOutput").ap()
    P = Prog(nc)
    NT1 = TOK + 128
    with ExitStack() as st_:
        sb, ps = _alloc(nc, st_)
        T, B = {}, {}
        T["xt"] = [sb("xt%d" % i, [128, D], F32) for i in range(2)]
        B["xt"] = [Buf() for i in range(2)]
        T["hb"] = [sb("hb%d" % i, [128, D], BF16) for i in range(2)]
        B["hb"] = [Buf() for i in range(2)]
        T["junk"] = sb("junk", [128, D], BF16); B["junk"] = Buf()
        T["st"] = sb("st", [128, 3 * 17], F32); B["st"] = Buf()
        cst = sb("cst", [128, 128], BF16); B["const"] = Buf()
        T["ident"] = cst[:, 0:128]
        pm = sb("pm", [128, 12 * 128], BF16)
        poolw = sb("poolw", [128, 512], BF16)
        vec = sb("vec", [128, 32], F32)
        gpre = sb("gpre", [128, 8], F32)
        gpost = sb("gpost", [128, D], F32)
        hT = sb("hT", [128, 8, NT1], BF16); hTB = Buf()
        ypT = sb("ypT", [128, 4, TOK], BF16); ypB = Buf()
        ycT = sb("ycT", [128, 4, TOK], BF16); ycB = Buf()
        ysT = sb("ysT", [128, 4, TOK], BF16); ysB = Buf()
        wch = [sb("wch%d" % i, [128, 8, 512], BF16) for i in range(2)]; wchB = [Buf() for _ in range(2)]
        ph = ExitStack()
        sb = lambda n, s_, d: ph.enter_context(nc.sbuf_tensor(n, s_, d))
        pvt = sb("pvt", [128, 17, 512], BF16); pvtB = Buf()
        spg = sb("spg", [128, 4, TOK], BF16); spgB = Buf()
        plb = [sb("plb%d" % i, [128, 512], BF16) for i in range(2)]; plbB = [Buf() for _ in range(2)]
        T["tp"] = ps("tp", [128, 8, 128], BF16); B["tp"] = Buf()
        PJ = [ps("PJ%d" % i, [128, 512], F32) for i in range(6)]; PJB = [Buf() for _ in range(6)]
        pjk = [0]

        def pj():
            i = pjk[0] % 6
            pjk[0] += 1
            return PJ[i], PJB[i]

        P.dma(lambda e: e.dma_start(out=cst[:], in_=cst_d), writes=[B["const"]], q="pool")
        P.dma(lambda e: e.dma_start(out=pm[:], in_=pm_d), writes=[B["const"]], q="pool")
        P.dma(lambda e: e.dma_start(out=poolw[:], in_=poolw_d), writes=[B["const"]], q="pool")
        P.dma(lambda e: e.dma_start(out=vec[:], in_=vec_d), writes=[B["const"]])
        P.dma(lambda e: e.dma_start(out=gpre[:], in_=gpre_d), writes=[B["const"]])
        P.dma(lambda e: e.dma_start(out=gpost[:], in_=gpost_d), writes=[B["const"]])
        P.dma(lambda e: e.dma_start(out=wch[0][:, :, :], in_=wpool_d[:, 0:512].rearrange("(kc p) n -> p kc n", p=128)), writes=[wchB[0]], q="pool")
        P.dma(lambda e: e.dma_start(out=wch[1][:, :, :], in_=wpool_d[:, 512:1024].rearrange("(kc p) n -> p kc n", p=128)), writes=[wchB[1]], q="pool")
        for i in range(4):
            P.dma(lambda e, i=i: e.dma_start(out=ysT[:, i, :], in_=ysb_d[i]), writes=[ysB], q="pool")
        for tb in range(17):
            emit_norm_block(P, nc, T, B, (lambda tb=tb: xo[tb * 128:(tb + 1) * 128, :]), tb, hT, hTB, tb * 128, gpre, tb % 2)
        for blk in range(17):
            z, zB = pj()
            for kc in range(8):
                P.op("pe", lambda e, z=z, kc=kc, blk=blk: e.matmul(z[:], hT[:, kc, blk * 128:(blk + 1) * 128], wch[0][:, kc, :], start=(kc == 0), stop=(kc == 7)),
                     reads=[hTB, wchB[0]], writes=[zB])
            P.op("dve", lambda e, z=z, blk=blk: e.tensor_copy(out=pvt[:, blk, :], in_=z[:]), reads=[zB], writes=[pvtB])
        for cc in range(4):
            for tc in range(4):
                z, zB = pj()
                for kc in range(8):
                    P.op("pe", lambda e, z=z, kc=kc, cc=cc, tc=tc: e.matmul(z[:], wch[1][:, kc, cc * 128:(cc + 1) * 128], hT[:, kc, 128 + tc * 512:128 + (tc + 1) * 512], start=(kc == 0), stop=(kc == 7)),
                         reads=[hTB, wchB[1]], writes=[zB])
                P.op("act", lambda e, z=z, cc=cc, tc=tc: e.activation(out=spg[:, cc, tc * 512:(tc + 1) * 512], in_=z[:], func=AF.Silu), reads=[zB], writes=[spgB])
        def load_conv(u):
            P.dma(lambda e: e.dma_start(out=wch[u % 2][:, :, :], in_=wconv_d[u].rearrange("(kc p) n -> p kc n", p=128)), writes=[wchB[u % 2]], q="pool")
        kk = 0
        for g in range(4):
            for tc in range(4):
                z, zB = pj()
                for j in range(4):
                    b = 1 + 4 * tc + j
                    first = (b == 1)
                    pmc = pm[:, (3 * g + (2 if first else 0)) * 128:(3 * g + (2 if first else 0) + 1) * 128]
                    pmp = pm[:, (3 * g + 1) * 128:(3 * g + 2) * 128]
                    P.op("pe", lambda e, z=z, j=j, b=b, g=g, pmc=pmc: e.matmul(z[:, j * 128:(j + 1) * 128], pvt[:, b, g * 128:(g + 1) * 128], pmc, start=True, stop=False),
                         reads=[pvtB, B["const"]], writes=[zB])
                    P.op("pe", lambda e, z=z, j=j, b=b, g=g, pmp=pmp: e.matmul(z[:, j * 128:(j + 1) * 128], pvt[:, b - 1, g * 128:(g + 1) * 128], pmp, start=False, stop=True),
                         reads=[pvtB, B["const"]], writes=[zB])
                pi = kk % 2; kk += 1
                P.op("act", lambda e, z=z, pi=pi: e.activation(out=plb[pi][:], in_=z[:], func=AF.Copy), reads=[zB], writes=[plbB[pi]])
                z2, z2B = pj()
                P.op("pe", lambda e, z2=z2, pi=pi, g=g: e.matmul(z2[:], poolw[:, g * 128:(g + 1) * 128], plb[pi][:], start=True, stop=True),
                     reads=[plbB[pi], B["const"]], writes=[z2B])
                sl = slice(tc * 512, (tc + 1) * 512)
                P.op("dve", lambda e, z2=z2, g=g, sl=sl: e.scalar_tensor_tensor(out=ypT[:, g, sl], in0=z2[:], scalar=vec[:, g:g + 1], in1=spg[:, g, sl], op0=ALU.mult, op1=ALU.mult),
                     reads=[z2B, spgB, B["const"]], writes=[ypB])
        P.barrier(); ph.close(); ph = ExitStack()
        sb = lambda n, s_, d: ph.enter_context(nc.sbuf_tensor(n, s_, d))
        cxT = sb("cxT", [128, NT1], F32); gcT = sb("gcT", [128, NT1], F32)
        gbT = sb("gbT", [128, TOK], F32); scg = sb("scg", [128, TOK], F32)
        zz = sb("zz", [128, NT1], F32); acc = sb("acc", [128, TOK], F32)
        cxB, gcB, gbB, scgB, zzB, accB = Buf(), Buf(), Buf(), Buf(), Buf(), Buf()
        load_conv(0)
        for u in range(4):
            if u + 1 < 4:
                load_conv(u + 1)
            w_, wB_ = wch[u % 2], wchB[u % 2]
            for which, dst, dB, c_off in ((0, cxT, cxB, 0), (2, gcT, gcB, 256)):
                for tcx in range(5):
                    t0, t1 = (0, 128) if tcx == 0 else (128 + (tcx - 1) * 512, 128 + tcx * 512)
                    z, zB = pj()
                    for kc in range(8):
                        P.op("pe", lambda e, z=z, kc=kc, t0=t0, t1=t1, c_off=c_off, w_=w_: e.matmul(z[:, 0:t1 - t0], w_[:, kc, c_off:c_off + 128], hT[:, kc, t0:t1], start=(kc == 0), stop=(kc == 7)),
                             reads=[hTB, wB_], writes=[zB])
                    P.op("act", lambda e, z=z, t0=t0, t1=t1, dst=dst: e.activation(out=dst[:, t0:t1], in_=z[:, 0:t1 - t0], func=AF.Copy), reads=[zB], writes=[dB])
            for c_off, dst, dB, fn in ((128, gbT, gbB, AF.Copy), (384, scg, scgB, AF.Silu)):
                for tc in range(4):
                    z, zB = pj()
                    for kc in range(8):
                        P.op("pe", lambda e, z=z, kc=kc, tc=tc, c_off=c_off, w_=w_: e.matmul(z[:], w_[:, kc, c_off:c_off + 128], hT[:, kc, 128 + tc * 512:128 + (tc + 1) * 512], start=(kc == 0), stop=(kc == 7)),
                             reads=[hTB, wB_], writes=[zB])
                    P.op("act", lambda e, z=z, tc=tc, dst=dst, fn=fn: e.activation(out=dst[:, tc * 512:(tc + 1) * 512], in_=z[:], func=fn), reads=[zB], writes=[dB])
            w0 = vec[:, 4 + u:5 + u]; w1 = vec[:, 8 + u:9 + u]; w2 = vec[:, 12 + u:13 + u]; cb = vec[:, 16 + u:17 + u]
            P.op("pool", lambda e: e.tensor_tensor(out=zz[:], in0=gcT[:], in1=cxT[:], op=ALU.mult), reads=[gcB, cxB], writes=[zzB])
            P.op("dve", lambda e, w2=w2, cb=cb: e.tensor_scalar(out=acc[:], in0=zz[:, 128:NT1], scalar1=w2, scalar2=cb, op0=ALU.mult, op1=ALU.add), reads=[zzB, B["const"]], writes=[accB])
            P.op("dve", lambda e, w1=w1: e.scalar_tensor_tensor(out=acc[:], in0=zz[:, 127:NT1 - 1], scalar=w1, in1=acc[:], op0=ALU.mult, op1=ALU.add), reads=[zzB, accB, B["const"]], writes=[accB])
            P.op("dve", lambda e, w0=w0: e.scalar_tensor_tensor(out=acc[:], in0=zz[:, 126:NT1 - 2], scalar=w0, in1=acc[:], op0=ALU.mult, op1=ALU.add), reads=[zzB, accB, B["const"]], writes=[accB])
            P.op("pool", lambda e: e.tensor_tensor(out=acc[:], in0=acc[:], in1=gbT[:], op=ALU.mult), reads=[accB, gbB], writes=[accB])
            P.op("dve", lambda e, u=u: e.tensor_tensor(out=ycT[:, u, :], in0=acc[:], in1=scg[:], op=ALU.mult), reads=[accB, scgB], writes=[ycB])
        P.barrier(); ph.close(); ph = ExitStack()
        sb = lambda n, s_, d: ph.enter_context(nc.sbuf_tensor(n, s_, d))
        mT = sb("mT", [128, 8, TOK], BF16); mTB = Buf()
        wmg = [sb("wmg%d" % i, [128, 8, 384], BF16) for i in range(2)]; wmgB = [Buf() for _ in range(2)]
        wbr = [sb("wbr%d" % i, [128, 4, 384], BF16) for i in range(2)]; wbrB = [Buf() for _ in range(2)]
        gsb = [sb("gsb%d" % i, [128, 512], F32) for i in range(2)]; gsbB = [Buf() for _ in range(2)]
        mac = [sb("mac%d" % i, [128, 512], F32) for i in range(2)]; macB = [Buf() for _ in range(2)]
        tmpb = [sb("tmpb%d" % i, [128, 512], F32) for i in range(2)]; tmpB = [Buf() for _ in range(2)]
        woutb = sb("woutb", [128, 8, D], BF16); woutB = Buf()

        def load_m(dcn):
            P.dma(lambda e: e.dma_start(out=wmg[dcn % 2][:, :, :], in_=wmg_d[dcn].rearrange("(kc p) n -> p kc n", p=128)), writes=[wmgB[dcn % 2]], q="pool")
            P.dma(lambda e: e.dma_start(out=wbr[dcn % 2][:, :, :], in_=wbr_d[dcn].rearrange("(kc p) n -> p kc n", p=128)), writes=[wbrB[dcn % 2]], q="pool")
        load_m(0)
        P.dma(lambda e: e.dma_start(out=woutb[:, :, :], in_=wout_d.rearrange("(kc p) n -> p kc n", p=128)), writes=[woutB], q="pool")
        yTs = [(ypT, ypB), (ycT, ycB), (ysT, ysB)]
        gk = 0
        for dcn in range(8):
            if dcn + 1 < 8:
                load_m(dcn + 1)
            wm_, wmB_ = wmg[dcn % 2], wmgB[dcn % 2]
            wb_, wbB_ = wbr[dcn % 2], wbrB[dcn % 2]
            for tc in range(4):
                sl = slice(tc * 512, (tc + 1) * 512)
                mi = (dcn * 4 + tc) % 2
                for n in range(3):
                    zg, zgB = pj()
                    for kc in range(8):
                        P.op("pe", lambda e, zg=zg, kc=kc, n=n, tc=tc, wm_=wm_: e.matmul(zg[:], wm_[:, kc, n * 128:(n + 1) * 128], hT[:, kc, 128 + tc * 512:128 + (tc + 1) * 512], start=(kc == 0), stop=(kc == 7)),
                             reads=[hTB, wmB_], writes=[zgB])
                    gi = gk % 2; gk += 1
                    P.op("act", lambda e, zg=zg, gi=gi: e.activation(out=gsb[gi][:], in_=zg[:], func=AF.Sigmoid), reads=[zgB], writes=[gsbB[gi]])
                    zp, zpB = pj()
                    yT_, yB_ = yTs[n]
                    for wc in range(4):
                        P.op("pe", lambda e, zp=zp, wc=wc, n=n, sl=sl, wb_=wb_, yT_=yT_: e.matmul(zp[:], wb_[:, wc, n * 128:(n + 1) * 128], yT_[:, wc, sl], start=(wc == 0), stop=(wc == 3)),
                             reads=[yB_, wbB_], writes=[zpB])
                    if n == 0:
                        P.op("dve", lambda e, zp=zp, gi=gi, mi=mi: e.tensor_tensor(out=mac[mi][:], in0=zp[:], in1=gsb[gi][:], op=ALU.mult), reads=[zpB, gsbB[gi]], writes=[macB[mi]])
                    else:
                        P.op("dve", lambda e, zp=zp, gi=gi, mi=mi: e.tensor_tensor(out=tmpb[mi][:], in0=zp[:], in1=gsb[gi][:], op=ALU.mult), reads=[zpB, gsbB[gi]], writes=[tmpB[mi]])
                        if n == 1:
                            P.op("pool", lambda e, mi=mi: e.tensor_tensor(out=mac[mi][:], in0=mac[mi][:], in1=tmpb[mi][:], op=ALU.add), reads=[macB[mi], tmpB[mi]], writes=[macB[mi]])
                        else:
                            P.op("pool", lambda e, mi=mi, dcn=dcn, sl=sl: e.tensor_tensor(out=mT[:, dcn, sl], in0=mac[mi][:], in1=tmpb[mi][:], op=ALU.add), reads=[macB[mi], tmpB[mi]], writes=[mTB])
        st2 = sb("st2", [128, 4 * NTB], F32); st2B = Buf()
        ot = [sb("ot%d" % i, [128, D], F32) for i in range(2)]; otB = [Buf() for _ in range(2)]
        for tb in range(NTB):
            za, zaB = pj()
            zb, zbB = pj()
            for half, (z, zB) in enumerate(((za, zaB), (zb, zbB))):
                for kc in range(8):
                    P.op("pe", lambda e, z=z, kc=kc, tb=tb, half=half: e.matmul(z[:], mT[:, kc, tb * 128:(tb + 1) * 128], woutb[:, kc, half * 512:(half + 1) * 512], start=(kc == 0), stop=(kc == 7)),
                         reads=[mTB, woutB], writes=[zB])
            c = 4 * tb
            P.op("act", lambda e, za=za, c=c: e.activation(out=T["junk"][:, 0:512], in_=za[:], func=AF.Square, accum_out=st2[:, c:c + 1]), reads=[zaB], writes=[B["junk"], st2B])
            P.op("act", lambda e, zb=zb, c=c: e.activation(out=T["junk"][:, 512:1024], in_=zb[:], func=AF.Square, accum_out=st2[:, c + 1:c + 2]), reads=[zbB], writes=[B["junk"], st2B])
            P.op("dve", lambda e, c=c: e.tensor_tensor(out=st2[:, c + 2:c + 3], in0=st2[:, c:c + 1], in1=st2[:, c + 1:c + 2], op=ALU.add), reads=[st2B], writes=[st2B])
            P.op("act", lambda e, c=c: e.activation(out=st2[:, c + 3:c + 4], in_=st2[:, c + 2:c + 3], func=AF.Ln, bias=EPS, scale=1.0 / D), reads=[st2B], writes=[st2B])
            P.op("act", lambda e, c=c: e.activation(out=st2[:, c + 2:c + 3], in_=st2[:, c + 3:c + 4], func=AF.Exp, scale=-0.5), reads=[st2B], writes=[st2B])
            oi = tb % 2
            for half, (z, zB) in enumerate(((za, zaB), (zb, zbB))):
                hs = slice(half * 512, (half + 1) * 512)
                P.op("dve", lambda e, z=z, c=c, hs=hs, oi=oi: e.scalar_tensor_tensor(out=ot[oi][:, hs], in0=z[:], scalar=st2[:, c + 2:c + 3], in1=gpost[:, hs], op0=ALU.mult, op1=ALU.mult),
                     reads=[zB, st2B, B["const"]], writes=[otB[oi]])
            P.dma(lambda e, tb=tb, oi=oi: e.dma_start(out=T["xt"][oi][:], in_=xo[(tb + 1) * 128:(tb + 2) * 128, :]), writes=[B["xt"][oi]])
            P.op("pool", lambda e, oi=oi, tb=tb: e.tensor_tensor(out=ot[oi][:], in0=ot[oi][:], in1=T["xt"][oi][:], op=ALU.add), reads=[otB[oi], B["xt"][oi]], writes=[otB[oi]])
            P.dma(lambda e, tb=tb, oi=oi: e.dma_start(out=xn_d[tb * 128:(tb + 1) * 128, :], in_=ot[oi][:]), reads=[otB[oi]], is_out=True)
        P.emit()
        ph.close()
    return nc


PAIRS = [[0, 1], [2, 3], [4, 5], [6, 7]]
NL = 2


def build_fused(nlayers=NL, dbg=False):
    nc = bass.Bass("TRN2", target_bir_lowering=False)
    dt_in = lambda n, s: nc.dram_tensor(n, s, F32, kind="ExternalInput").ap()
    xo = dt_in("xo", [TOK, D])
    gpre_d = dt_in("gpre_d", [NL, 128, 8])
    wq_d = dt_in("wq", [NL, D, 1024])
    cst_d = dt_in("cst_d", [128, 640])
    pm_d = dt_in("pm_d", [128, 12 * 128])
    flg_d = dt_in("flg_d", [128, 4])
    wpool_d = dt_in("wpool", [NL, D, 1024])
    poolw_d = dt_in("poolw_d", [NL, 128, 512])
    vec_d = dt_in("vec_d", [NL, 128, 32])
    wconv_d = dt_in("wconv", [NL, 4, D, 512])
    wmg_d = dt_in("wmg", [NL, 8, D, 384])
    wbr_d = dt_in("wbr", [NL, 8, 512, 384])
    wout_d = dt_in("wout", [NL, D, D])
    gpost_d = dt_in("gpost_d", [NL, 128, D])
    xn_d = nc.dram_tensor("xn", [TOK, D], F32, kind="ExternalOutput").ap()
    ib_h = [nc.dram_tensor("ib_h%d" % j, [1024, 512], BF16) for j in range(4)]
    ob_h = [nc.dram_tensor("ob_h%d" % j, [2 * 1024, 512], BF16) for j in range(4)]
    ib_y = nc.dram_tensor("ib_y", [256, S], BF16)
    ob_y = nc.dram_tensor("ob_y", [2 * 256, S], BF16)
    ibhB = [Buf() for _ in range(4)]; obhB = [Buf() for _ in range(4)]; ibyB = Buf(); obyB = Buf()
    P = Prog(nc)
    NT1 = TOK + 128
    with ExitStack() as st_:
        sb, ps = _alloc(nc, st_)
        T, B = {}, {}
        xres = sb("xres", [128, NTB, D], F32); xresB = [Buf() for _ in range(NTB)]
        B["junk"] = Buf()
        st = sb("st", [128, 3 * 16 * NL], F32); stB = Buf()
        st2 = sb("st2", [128, 4 * NTB * NL], F32); st2B = Buf()
        cst = sb("cst", [128, 640], BF16); B["const"] = Buf()
        ident = cst[:, 0:128]; negtri = cst[:, 128:256]; negones = cst[:, 256:384]; cmask = cst[:, 384:512]; cmask2 = cst[:, 384:640].rearrange("p (h q) -> p h q", h=2)
        flg = sb("flg", [128, 4], F32)
        gpre = sb("gpre", [128, NL, 8], F32)
        vec = sb("vec", [128, NL, 32], F32)
        tpf = ps("tpf", [128, 512], F32); tpB = Buf()
        tp = tpf[:].bitcast(BF16).rearrange("p (a b) -> p a b", a=8)
        KBall = ps("KBall", [128, 7, 512], F32)
        KB = [KBall[:, i, :] for i in range(7)]; KBB = [Buf() for _ in range(7)]
        pjk = [0]

        def pj():
            i = pjk[0] % 6
            pjk[0] += 1
            return KB[i], KBB[i]

        P.dma(lambda e: e.dma_start(out=cst[:], in_=cst_d), writes=[B["const"]], q="pool")
        P.dma(lambda e: e.dma_start(out=flg[:], in_=flg_d), writes=[B["const"]])
        for l in range(NL):
            P.dma(lambda e, l=l: e.dma_start(out=gpre[:, l, :], in_=gpre_d[l]), writes=[B["const"]])
            P.dma(lambda e, l=l: e.dma_start(out=vec[:, l, :], in_=vec_d[l]), writes=[B["const"]])
        for tb in range(NTB):
            P.dma(lambda e, tb=tb: e.dma_start(out=xres[:, tb, :], in_=xo[tb * 128:(tb + 1) * 128, :]), writes=[xresB[tb]])

        for l in range(nlayers):
            phW = ExitStack()
            wqb = phW.enter_context(nc.sbuf_tensor("wqb_%d" % l, [128, 8, 1024], BF16)); wqB = Buf()
            P.dma(lambda e, l=l: e.dma_start(out=wqb[:], in_=wq_d[l].rearrange("(kc p) n -> p kc n", p=128)), writes=[wqB], q="pool")
            ph = ExitStack()
            sbp = lambda n, s_, d, ph=ph: ph.enter_context(nc.sbuf_tensor(n, s_, d))
            hTn = sbp("hTn%d" % l, [128, 8, TOK], BF16); hTnB = [Buf() for _ in range(4)]
            T["hb"] = [sbp("hb%d_%d" % (i, l), [128, D], BF16) for i in range(2)]
            B["hb"] = [Buf() for i in range(2)]
            for tb in range(NTB):
                c = 3 * (16 * l + tb)
                hb, hbB = T["hb"][tb % 2], B["hb"][tb % 2]
                P.op("act", lambda e, tb=tb, c=c: e.activation(out=KBall[:, 5:7, :].rearrange("p a b -> p (a b)"), in_=xres[:, tb, :], func=AF.Square, accum_out=st[:, c:c + 1]),
                     reads=[xresB[tb]], writes=[KBB[5], KBB[6], stB])
                P.op("act", lambda e, c=c: e.activation(out=st[:, c + 1:c + 2], in_=st[:, c:c + 1], func=AF.Ln, bias=EPS, scale=1.0 / D), reads=[stB], writes=[stB])
                P.op("act", lambda e, c=c: e.activation(out=st[:, c + 2:c + 3], in_=st[:, c + 1:c + 2], func=AF.Exp, scale=-0.5), reads=[stB], writes=[stB])
                P.op("dve", lambda e, tb=tb, c=c, hb=hb: e.tensor_scalar_mul(out=hb[:], in0=xres[:, tb, :], scalar1=st[:, c + 2:c + 3]), reads=[xresB[tb], stB], writes=[hbB])
                for dc in range(8):
                    P.op("pe", lambda e, dc=dc, hb=hb: e.transpose(tp[:, dc, :], hb[:, dc * 128:(dc + 1) * 128], ident), reads=[hbB, B["const"]], writes=[tpB])
                for dc in range(8):
                    P.op("dve", lambda e, dc=dc, tb=tb, l=l: e.tensor_scalar_mul(out=hTn[:, dc, tb * 128:(tb + 1) * 128], in0=tp[:, dc, :], scalar1=gpre[:, l, dc:dc + 1]),
                         reads=[tpB, B["const"]], writes=[hTnB[tb // 4]])
                if tb % 4 == 3:
                    qd = tb // 4
                    for dc in range(8):
                        P.dma(lambda e, dc=dc, qd=qd: e.dma_start(out=ib_h[qd].ap()[dc * 128:(dc + 1) * 128, :], in_=hTn[:, dc, qd * 512:(qd + 1) * 512]), reads=[hTnB[qd]], writes=[ibhB[qd]])
                    P.dma(lambda e, qd=qd: e.collective_compute("AllGather", ALU.bypass, replica_groups=PAIRS, ins=[ib_h[qd].ap().opt()], outs=[ob_h[qd].ap().opt()]),
                          reads=[ibhB[qd]], writes=[obhB[qd]], q="pool", inc=1)
            P.barrier(); ph.close()

            ph = ExitStack()
            sbp = lambda n, s_, d, ph=ph: ph.enter_context(nc.sbuf_tensor(n, s_, d))
            L = "_%d" % l
            hch = [sbp("hch%d" % i + L, [128, 8, 512], BF16) for i in range(2)]; hchB = [Buf() for _ in range(2)]
            qT = sbp("qT" + L, [128, 2, S], BF16); kT = sbp("kT" + L, [128, 2, S], BF16)
            sgT = sbp("sgT" + L, [128, 2, S], BF16); vt = sbp("vt" + L, [128, 32, 256], BF16)
            qB = [Buf() for _ in range(8)]; kB = [Buf() for _ in range(8)]; sgB = [Buf() for _ in range(8)]; vB = [Buf() for _ in range(8)]
            ysb = sbp("ysbT" + L, [128, 2, S], BF16); ysbB = Buf()
            ee = [sbp("ee%d" % i + L, [128, 2, 512], F32) for i in range(2)]; eeB = [Buf() for _ in range(2)]
            spp = [sbp("spp%d" % i + L, [128, 2, 512], BF16) for i in range(3)]; sppB = [Buf() for _ in range(3)]
            AT = [sbp("AT%d" % i + L, [128, 2, 512], BF16) for i in range(2)]; ATB = [Buf() for _ in range(2)]
            Sf = sbp("Sf" + L, [128, 2, 512], F32); SfB = Buf()
            Sb = [sbp("Sb%d" % i + L, [128, 2, 512], BF16) for i in range(2)]; SbB = [Buf() for _ in range(2)]
            Z1p = [KBall[:, 0:2, :], KBall[:, 2:4, :]]; Z1pB = [Buf(), Buf()]
            Z2p = KBall[:, 4:6, :]; Z2pB = Buf()
            OOp = KB[6]; OOpB = KBB[6]
            Z1, Z1B = [KB[0], KB[2]], Z1pB
            def a2_groups(tc):
                hc, hcB = hch[tc % 2], hchB[tc % 2]
                sl = slice(tc * 512, (tc + 1) * 512)
                gl = []
                for cc in range(6):
                    for half in range(4):
                        def g(cc=cc, half=half):
                            for kc in range(2 * half, 2 * half + 2):
                                P.op("pe", lambda e, kc=kc: e.matmul(tpf[:], wqb[:, kc, cc * 128:(cc + 1) * 128], hc[:, kc, :], start=(kc == 0), stop=(kc == 7)),
                                     reads=[wqB, hcB], writes=[tpB])
                            if half == 3:
                                if cc < 2:
                                    P.op("dve", lambda e: e.tensor_scalar_mul(out=qT[:, cc, sl], in0=tpf[:], scalar1=0.125), reads=[tpB], writes=[qB[tc]])
                                elif cc < 4:
                                    P.op("dve", lambda e: e.tensor_copy(out=kT[:, cc - 2, sl], in_=tpf[:]), reads=[tpB], writes=[kB[tc]])
                                else:
                                    P.op("act", lambda e: e.activation(out=sgT[:, cc - 4, sl], in_=tpf[:], func=AF.Silu), reads=[tpB], writes=[sgB[tc]])
                        gl.append(g)
                for jb in range(4):
                    for half in range(2):
                        def g(jb=jb, half=half):
                            tb = 4 * tc + jb
                            for kc in range(4 * half, 4 * half + 4):
                                P.op("pe", lambda e, kc=kc: e.matmul(tpf[:, 0:256], hc[:, kc, jb * 128:(jb + 1) * 128], wqb[:, kc, 768:1024], start=(kc == 0), stop=(kc == 7)),
                                     reads=[wqB, hcB], writes=[tpB])
                            if half == 1:
                                P.op("dve", lambda e: e.tensor_copy(out=vt[:, tb, :], in_=tpf[:, 0:256]), reads=[tpB], writes=[vB[tc]])
                        gl.append(g)
                return gl

            def load_hch(tc):
                r, qd = tc // 4, tc % 4
                P.dma(lambda e: e.dma_start(out=hch[tc % 2][:, :, :], in_=ob_h[qd].ap()[r * 1024:(r + 1) * 1024, :].rearrange("(k p) t -> p k t", p=128)),
                      reads=[obhB[qd]], writes=[hchB[tc % 2]])
            load_hch(0)
            load_hch(1)
            for g in a2_groups(0):
                g()
            prs = []
            for G in range(8):
                for hp in range(2):
                    for kb in range(4 * G + 3, -1, -1):
                        prs.append((hp, G, kb))
            nS = {}
            hsl = [slice(0, 64), slice(64, 128)]

            def s1(t, pr):
                hp, G, kb = pr
                r = kb - 4 * G if kb >= 4 * G else 0
                c0 = 128 * r
                zi, ei, si = t % 2, t % 2, t % 3
                for hh in range(2):
                    P.op("pe", lambda e, hh=hh: e.matmul(Z1p[zi][:, hh, c0:512], kT[hsl[hh], hp, kb * 128:(kb + 1) * 128], qT[hsl[hh], hp, G * 512 + c0:(G + 1) * 512], start=True, stop=True),
                         reads=[kB[kb // 4], qB[G]], writes=[Z1pB[zi]])
                P.op("act", lambda e: e.activation(out=ee[ei][:, :, c0:512], in_=Z1p[zi][:, :, c0:512], func=AF.Exp), reads=[Z1pB[zi]], writes=[eeB[ei]])
                P.op("act", lambda e: e.activation(out=spp[si][:, :, c0:512], in_=ee[ei][:, :, c0:512], func=AF.Ln, bias=1.0, scale=1.0), reads=[eeB[ei]], writes=[sppB[si]])
                if kb >= 4 * G:
                    P.op("pool", lambda e: e.tensor_tensor(out=spp[si][:, :, c0:c0 + 128], in0=spp[si][:, :, c0:c0 + 128], in1=cmask2, op=ALU.mult),
                         reads=[sppB[si], B["const"]], writes=[sppB[si]])

            def s2(t, pr):
                hp, G, kb = pr
                diag = kb >= 4 * G
                r = kb - 4 * G if diag else 0
                c0 = 128 * r
                si, ai = t % 3, t % 2
                top = (kb == 4 * G + 3)
                n = nS.get((hp, G), 0)
                nS[(hp, G)] = n + 1
                if top:
                    P.op("dve", lambda e: e.memset(Sf[:], 0.0), writes=[SfB])
                for hh in range(2):
                    P.op("pe", lambda e, hh=hh: e.matmul(Z2p[:, hh, c0:512], kT[hsl[hh], hp, kb * 128:(kb + 1) * 128], qT[hsl[hh], hp, G * 512 + c0:(G + 1) * 512], start=True, stop=False),
                         reads=[kB[kb // 4], qB[G]], writes=[Z2pB])
                for hh in range(2):
                    P.op("pe", lambda e, hh=hh: e.matmul(Z2p[:, hh, c0:512], negtri, spp[si][:, hh, c0:512], start=False, stop=top),
                         reads=[sppB[si], B["const"]], writes=[Z2pB])
                if not top:
                    pjx = (n - 1) % 2
                    for hh in range(2):
                        P.op("pe", lambda e, hh=hh: e.matmul(Z2p[:, hh, c0:512], negones, Sb[pjx][:, hh, c0:512], start=False, stop=True),
                             reads=[SbB[pjx], B["const"]], writes=[Z2pB])
                if kb > 0:
                    P.op("dve", lambda e: e.tensor_tensor(out=Sb[n % 2][:, :, c0:512], in0=Sf[:, :, c0:512], in1=spp[si][:, :, c0:512], op=ALU.add),
                         reads=[SfB, sppB[si]], writes=[SbB[n % 2]])
                    if c0 > 0:
                        P.op("pool", lambda e: e.memset(Sb[n % 2][:, :, 0:c0], 0.0), writes=[SbB[n % 2]])
                    P.op("pool", lambda e: e.tensor_tensor(out=Sf[:, :, c0:512], in0=Sf[:, :, c0:512], in1=spp[si][:, :, c0:512], op=ALU.add),
                         reads=[SfB, sppB[si]], writes=[SfB])
                P.op("act", lambda e: e.activation(out=AT[ai][:, :, c0:512], in_=Z2p[:, :, c0:512], func=AF.Exp), reads=[Z2pB], writes=[ATB[ai]])
                if diag:
                    P.op("pool", lambda e: e.tensor_tensor(out=AT[ai][:, :, c0:c0 + 128], in0=AT[ai][:, :, c0:c0 + 128], in1=cmask2, op=ALU.mult),
                         reads=[ATB[ai], B["const"]], writes=[ATB[ai]])
                if top:
                    P.op("pool", lambda e: e.memset(AT[ai][:, :, 0:c0], 0.0), writes=[ATB[ai]])

            def s3(t, pr):
                hp, G, kb = pr
                diag = kb >= 4 * G
                c0 = 128 * (kb - 4 * G) if diag else 0
                ai = t % 2
                top = (kb == 4 * G + 3)
                last = (kb == 0)
                o0 = 0 if top else c0
                for hh in range(2):
                    P.op("pe", lambda e, hh=hh: e.matmul(OOp[hsl[hh], o0:512], vt[:, kb, hp * 128 + hh * 64:hp * 128 + (hh + 1) * 64], AT[ai][:, hh, o0:512],
                                                         start=top, stop=last, tile_position=(0, 64 * hh)),
                         reads=[vB[kb // 4], ATB[ai]], writes=[OOpB])
                if last:
                    gs = slice(G * 512, (G + 1) * 512)
                    P.op("dve", lambda e: e.tensor_tensor(out=ysb[:, hp, gs], in0=OOp[:, :], in1=sgT[:, hp, gs], op=ALU.mult),
                         reads=[OOpB, sgB[G]], writes=[ysbB])

            sched = {}
            t_base = 0
            for G in range(8):
                nst = 2 * (4 * G + 4)
                if G + 1 < 8:
                    gl = a2_groups(G + 1)
                    for j, g in enumerate(gl):
                        st_i = t_base + (j * (nst - 1)) // len(gl)
                        sched.setdefault(st_i, []).append(g)
                    if G + 2 < 8:
                        sched.setdefault(t_base, []).append(lambda G=G: load_hch(G + 2))
                t_base += nst
            s1(0, prs[0])
            for t in range(len(prs)):
                for g in sched.get(t, []):
                    g()
                if t + 1 < len(prs):
                    s1(t + 1, prs[t + 1])
                s2(t, prs[t])
                if t > 0:
                    s3(t - 1, prs[t - 1])
            s3(len(prs) - 1, prs[-1])
            for hp in range(2):
                P.dma(lambda e, hp=hp: e.dma_start(out=ib_y.ap()[hp * 128:(hp + 1) * 128, :], in_=ysb[:, hp, :]), reads=[ysbB], writes=[ibyB])
            P.barrier(); ph.close(); phW.close()

            phB = ExitStack()
            sbB_ = lambda n, s_, d, phB=phB: phB.enter_context(nc.sbuf_tensor(n, s_, d))
            hT = sbB_("hT" + L, [128, 8, NT1], BF16); hTB = Buf()
            ypT = sbB_("ypT" + L, [128, 4, TOK], BF16); ypB = Buf()
            ycT = sbB_("ycT" + L, [128, 4, TOK], BF16); ycB = Buf()
            for qd in range(4):
                P.dma(lambda e, qd=qd: e.dma_start(out=hT[:, :, 128 + qd * 512:128 + (qd + 1) * 512], in_=ib_h[qd].ap().rearrange("(k p) t -> p k t", p=128)), reads=[ibhB[qd]], writes=[hTB])
            ph = ExitStack()
            sbp = lambda n, s_, d, ph=ph: ph.enter_context(nc.sbuf_tensor(n, s_, d))
            hal = sbp("hal" + L, [128, 8, 128], BF16); halB = Buf()
            P.dma(lambda e: e.dma_start(out=hal[:, :, :], in_=ob_h[3].ap()[0:1024, 384:512].rearrange("(k p) t -> p k t", p=128)), reads=[obhB[3]], writes=[halB])
            P.op("dve", lambda e: e.tensor_scalar_mul(out=hT[:, :, 0:128], in0=hal[:, :, :], scalar1=flg[:, 0:1]), reads=[halB, B["const"]], writes=[hTB])
            wch = [sbp("wch%d" % i + L, [128, 8, 512], BF16) for i in range(2)]; wchB = [Buf() for _ in range(2)]
            pvt = sbp("pvt" + L, [128, 17, 512], BF16); pvtB = Buf()
            spg = sbp("spg" + L, [128, 4, TOK], BF16); spgB = Buf()
            plb = [sbp("plb%d" % i + L, [128, 512], BF16) for i in range(2)]; plbB = [Buf() for _ in range(2)]
            poolw = sbp("poolw" + L, [128, 512], BF16); poolwB = Buf()
            pm = sbp("pm" + L, [128, 12 * 128], BF16)
            P.dma(lambda e: e.dma_start(out=pm[:], in_=pm_d), writes=[poolwB], q="pool")
            P.dma(lambda e, l=l: e.dma_start(out=poolw[:], in_=poolw_d[l]), writes=[poolwB], q="pool")
            P.dma(lambda e, l=l: e.dma_start(out=wch[0][:, :, :], in_=wpool_d[l][:, 0:512].rearrange("(kc p) n -> p kc n", p=128)), writes=[wchB[0]], q="pool")
            P.dma(lambda e, l=l: e.dma_start(out=wch[1][:, :, :], in_=wpool_d[l][:, 512:1024].rearrange("(kc p) n -> p kc n", p=128)), writes=[wchB[1]], q="pool")
            for blk in range(17):
                z, zB = pj()
                for kc in range(8):
                    P.op("pe", lambda e, z=z, kc=kc, blk=blk: e.matmul(z[:], hT[:, kc, blk * 128:(blk + 1) * 128], wch[0][:, kc, :], start=(kc == 0), stop=(kc == 7)),
                         reads=[hTB, wchB[0]], writes=[zB])
                P.op("dve", lambda e, z=z, blk=blk: e.tensor_copy(out=pvt[:, blk, :], in_=z[:]), reads=[zB], writes=[pvtB])
            for cc in range(4):
                for tc in range(4):
                    z, zB = pj()
                    for kc in range(8):
                        P.op("pe", lambda e, z=z, kc=kc, cc=cc, tc=tc: e.matmul(z[:], wch[1][:, kc, cc * 128:(cc + 1) * 128], hT[:, kc, 128 + tc * 512:128 + (tc + 1) * 512], start=(kc == 0), stop=(kc == 7)),
                             reads=[hTB, wchB[1]], writes=[zB])
                    P.op("act", lambda e, z=z, cc=cc, tc=tc: e.activation(out=spg[:, cc, tc * 512:(tc + 1) * 512], in_=z[:], func=AF.Silu), reads=[zB], writes=[spgB])
            kk = 0
            for g in range(4):
                for tc in range(4):
                    z, zB = pj()
                    for jx in range(4):
                        b = 1 + 4 * tc + jx
                        first = (b == 1)
                        pmc = pm[:, (3 * g + (2 if first else 0)) * 128:(3 * g + (2 if first else 0) + 1) * 128]
                        pmp = pm[:, (3 * g + 1) * 128:(3 * g + 2) * 128]
                        P.op("pe", lambda e, z=z, jx=jx, b=b, g=g, pmc=pmc: e.matmul(z[:, jx * 128:(jx + 1) * 128], pvt[:, b, g * 128:(g + 1) * 128], pmc, start=True, stop=False),
                             reads=[pvtB, poolwB], writes=[zB])
                        P.op("pe", lambda e, z=z, jx=jx, b=b, g=g, pmp=pmp: e.matmul(z[:, jx * 128:(jx + 1) * 128], pvt[:, b - 1, g * 128:(g + 1) * 128], pmp, start=False, stop=True),
                             reads=[pvtB, poolwB], writes=[zB])
                    pi = kk % 2; kk += 1
                    P.op("act", lambda e, z=z, pi=pi: e.activation(out=plb[pi][:], in_=z[:], func=AF.Copy), reads=[zB], writes=[plbB[pi]])
                    z2, z2B = pj()
                    P.op("pe", lambda e, z2=z2, pi=pi, g=g: e.matmul(z2[:], poolw[:, g * 128:(g + 1) * 128], plb[pi][:], start=True, stop=True),
                         reads=[plbB[pi], poolwB], writes=[z2B])
                    sl = slice(tc * 512, (tc + 1) * 512)
                    P.op("dve", lambda e, z2=z2, g=g, sl=sl, l=l: e.scalar_tensor_tensor(out=ypT[:, g, sl], in0=z2[:], scalar=vec[:, l, g:g + 1], in1=spg[:, g, sl], op0=ALU.mult, op1=ALU.mult),
                         reads=[z2B, spgB, B["const"]], writes=[ypB])
            P.barrier(); ph.close()
            ph = ExitStack()
            sbp = lambda n, s_, d, ph=ph: ph.enter_context(nc.sbuf_tensor(n, s_, d))
            wch = [sbp("wcv%d" % i + L, [128, 8, 512], BF16) for i in range(2)]; wchB = [Buf() for _ in range(2)]
            cxT = sbp("cxT" + L, [128, NT1], F32); gcT = sbp("gcT" + L, [128, NT1], F32)
            gbT = sbp("gbT" + L, [128, TOK], F32); scg = sbp("scg" + L, [128, TOK], F32)
            zz = sbp("zz" + L, [128, NT1], F32); acc = sbp("acc" + L, [128, TOK], F32)
            cxB, gcB, gbB, scgB, zzB, accB = Buf(), Buf(), Buf(), Buf(), Buf(), Buf()

            def load_conv(u):
                P.dma(lambda e, u=u, l=l: e.dma_start(out=wch[u % 2][:, :, :], in_=wconv_d[l, u].rearrange("(kc p) n -> p kc n", p=128)), writes=[wchB[u % 2]], q="pool")
            load_conv(0)
            for u in range(4):
                if u + 1 < 4:
                    load_conv(u + 1)
                if u == 2:
                    P.dma(lambda e: e.collective_compute("AllGather", ALU.bypass, replica_groups=PAIRS, ins=[ib_y.ap().opt()], outs=[ob_y.ap().opt()]),
                          reads=[ibyB], writes=[obyB], q="pool", inc=1)
                w_, wB_ = wch[u % 2], wchB[u % 2]
                for which, dst, dB, c_off in ((0, cxT, cxB, 0), (2, gcT, gcB, 256)):
                    for tcx in range(5):
                        t0, t1 = (0, 128) if tcx == 0 else (128 + (tcx - 1) * 512, 128 + tcx * 512)
                        z, zB = pj()
                        for kc in range(8):
                            P.op("pe", lambda e, z=z, kc=kc, t0=t0, t1=t1, c_off=c_off, w_=w_: e.matmul(z[:, 0:t1 - t0], w_[:, kc, c_off:c_off + 128], hT[:, kc, t0:t1], start=(kc == 0), stop=(kc == 7)),
                                 reads=[hTB, wB_], writes=[zB])
                        P.op("act", lambda e, z=z, t0=t0, t1=t1, dst=dst: e.activation(out=dst[:, t0:t1], in_=z[:, 0:t1 - t0], func=AF.Copy), reads=[zB], writes=[dB])
                for c_off, dst, dB, fn in ((128, gbT, gbB, AF.Copy), (384, scg, scgB, AF.Silu)):
                    for tc in range(4):
                        z, zB = pj()
                        for kc in range(8):
                            P.op("pe", lambda e, z=z, kc=kc, tc=tc, c_off=c_off, w_=w_: e.matmul(z[:], w_[:, kc, c_off:c_off + 128], hT[:, kc, 128 + tc * 512:128 + (tc + 1) * 512], start=(kc == 0), stop=(kc == 7)),
                                 reads=[hTB, wB_], writes=[zB])
                        P.op("act", lambda e, z=z, tc=tc, dst=dst, fn=fn: e.activation(out=dst[:, tc * 512:(tc + 1) * 512], in_=z[:], func=fn), reads=[zB], writes=[dB])
                w0 = vec[:, l, 4 + u:5 + u]; w1 = vec[:, l, 8 + u:9 + u]; w2 = vec[:, l, 12 + u:13 + u]; cb = vec[:, l, 16 + u:17 + u]
                P.op("dve", lambda e: e.tensor_tensor(out=zz[:], in0=gcT[:], in1=cxT[:], op=ALU.mult), reads=[gcB, cxB], writes=[zzB])
                P.op("dve", lambda e, w2=w2, cb=cb: e.tensor_scalar(out=acc[:], in0=zz[:, 128:NT1], scalar1=w2, scalar2=cb, op0=ALU.mult, op1=ALU.add), reads=[zzB, B["const"]], writes=[accB])
                P.op("dve", lambda e, w1=w1: e.scalar_tensor_tensor(out=acc[:], in0=zz[:, 127:NT1 - 1], scalar=w1, in1=acc[:], op0=ALU.mult, op1=ALU.add), reads=[zzB, accB, B["const"]], writes=[accB])
                P.op("dve", lambda e, w0=w0: e.scalar_tensor_tensor(out=acc[:], in0=zz[:, 126:NT1 - 2], scalar=w0, in1=acc[:], op0=ALU.mult, op1=ALU.add), reads=[zzB, accB, B["const"]], writes=[accB])
                P.op("dve", lambda e: e.tensor_tensor(out=acc[:], in0=acc[:], in1=gbT[:], op=ALU.mult), reads=[accB, gbB], writes=[accB])
                P.op("dve", lambda e, u=u: e.tensor_tensor(out=ycT[:, u, :], in0=acc[:], in1=scg[:], op=ALU.mult), reads=[accB, scgB], writes=[ycB])
            P.barrier(); ph.close()
            ysT = sbB_("ysT" + L, [128, 4, TOK], BF16); ysB = Buf()
            ph = ExitStack()
            sbp = lambda n, s_, d, ph=ph: ph.enter_context(nc.sbuf_tensor(n, s_, d))
            ta = [sbp("ta%d" % i + L, [128, TOK], BF16) for i in range(2)]; taB = [Buf() for _ in range(2)]
            tb_ = [sbp("tb%d" % i + L, [128, TOK], BF16) for i in range(2)]; tbB = [Buf() for _ in range(2)]
            for i in range(4):
                r, hp = i // 2, i % 2
                row0 = r * 256 + hp * 128
                P.dma(lambda e, i=i, row0=row0: e.dma_start(out=ta[i % 2][:], in_=ob_y.ap()[row0:row0 + 128, 0:TOK]), reads=[obyB], writes=[taB[i % 2]])
                P.dma(lambda e, i=i, row0=row0: e.dma_start(out=tb_[i % 2][:], in_=ob_y.ap()[row0:row0 + 128, TOK:S]), reads=[obyB], writes=[tbB[i % 2]])
                P.op("dve", lambda e, i=i: e.tensor_scalar_mul(out=ta[i % 2][:], in0=ta[i % 2][:], scalar1=flg[:, 1:2]), reads=[taB[i % 2], B["const"]], writes=[taB[i % 2]])
                P.op("dve", lambda e, i=i: e.scalar_tensor_tensor(out=ysT[:, i, :], in0=tb_[i % 2][:], scalar=flg[:, 2:3], in1=ta[i % 2][:], op0=ALU.mult, op1=ALU.add),
                     reads=[taB[i % 2], tbB[i % 2], B["const"]], writes=[ysB])
            P.barrier(); ph.close()
            ph6 = ExitStack()
            mT = ph6.enter_context(nc.sbuf_tensor("mT" + L, [128, 8, TOK], BF16)); mTB = Buf()
            ph = ExitStack()
            sbp = lambda n, s_, d, ph=ph: ph.enter_context(nc.sbuf_tensor(n, s_, d))
            wmg = [sbp("wmg%d" % i + L, [128, 8, 384], BF16) for i in range(2)]; wmgB = [Buf() for _ in range(2)]
            wbr = [sbp("wbr%d" % i + L, [128, 4, 384], BF16) for i in range(1)] * 2; wbrB = [Buf()] * 2
            gsb = [sbp("gsb%d" % i + L, [128, 512], F32) for i in range(1)] * 2; gsbB = [Buf()] * 2
            mac = [sbp("mac%d" % i + L, [128, 512], F32) for i in range(1)] * 2; macB = [Buf()] * 2
            tmpb = [sbp("tmpb%d" % i + L, [128, 512], F32) for i in range(1)] * 2; tmpB = [Buf()] * 2

            def load_mg(dcn):
                P.dma(lambda e, dcn=dcn, l=l: e.dma_start(out=wmg[dcn % 2][:, :, :], in_=wmg_d[l, dcn].rearrange("(kc p) n -> p kc n", p=128)), writes=[wmgB[dcn % 2]], q="pool")

            def load_br(dcn):
                P.dma(lambda e, dcn=dcn, l=l: e.dma_start(out=wbr[dcn % 2][:, :, :], in_=wbr_d[l, dcn].rearrange("(kc p) n -> p kc n", p=128)), writes=[wbrB[dcn % 2]], q="pool")
            yTs = [(ypT, ypB), (ycT, ycB), (ysT, ysB)]
            gk = 0
            load_mg(0)
            for dcn in range(8):
                load_br(dcn)
                if dcn + 1 < 8:
                    load_mg(dcn + 1)
                wm_, wmB_ = wmg[dcn % 2], wmgB[dcn % 2]
                wb_, wbB_ = wbr[dcn % 2], wbrB[dcn % 2]
                for tc in range(4):
                    sl = slice(tc * 512, (tc + 1) * 512)
                    mi = (dcn * 4 + tc) % 2
                    for n in range(3):
                        zg, zgB = pj()
                        for kc in range(8):
                            P.op("pe", lambda e, zg=zg, kc=kc, n=n, tc=tc, wm_=wm_: e.matmul(zg[:], wm_[:, kc, n * 128:(n + 1) * 128], hT[:, kc, 128 + tc * 512:128 + (tc + 1) * 512], start=(kc == 0), stop=(kc == 7)),
                                 reads=[hTB, wmB_], writes=[zgB])
                        gi = gk % 2; gk += 1
                        P.op("act", lambda e, zg=zg, gi=gi: e.activation(out=gsb[gi][:], in_=zg[:], func=AF.Sigmoid), reads=[zgB], writes=[gsbB[gi]])
                        zp, zpB = pj()
                        yT_, yB_ = yTs[n]
                        for wc in range(4):
                            P.op("pe", lambda e, zp=zp, wc=wc, n=n, sl=sl, wb_=wb_, yT_=yT_: e.matmul(zp[:], wb_[:, wc, n * 128:(n + 1) * 128], yT_[:, wc, sl], start=(wc == 0), stop=(wc == 3)),
                                 reads=[yB_, wbB_], writes=[zpB])
                        if n == 0:
                            P.op("dve", lambda e, zp=zp, gi=gi, mi=mi: e.tensor_tensor(out=mac[mi][:], in0=zp[:], in1=gsb[gi][:], op=ALU.mult), reads=[zpB, gsbB[gi]], writes=[macB[mi]])
                        else:
                            P.op("dve", lambda e, zp=zp, gi=gi, mi=mi: e.tensor_tensor(out=tmpb[mi][:], in0=zp[:], in1=gsb[gi][:], op=ALU.mult), reads=[zpB, gsbB[gi]], writes=[tmpB[mi]])
                            if n == 1:
                                P.op("pool", lambda e, mi=mi: e.tensor_tensor(out=mac[mi][:], in0=mac[mi][:], in1=tmpb[mi][:], op=ALU.add), reads=[macB[mi], tmpB[mi]], writes=[macB[mi]])
                            else:
                                P.op("pool", lambda e, mi=mi, dcn=dcn, sl=sl: e.tensor_tensor(out=mT[:, dcn, sl], in0=mac[mi][:], in1=tmpb[mi][:], op=ALU.add), reads=[macB[mi], tmpB[mi]], writes=[mTB])
            if dbg and l == 0:
                dbg_d = nc.dram_tensor("dbg", [5, 128, NT1], F32, kind="ExternalOutput").ap()
                P.dma(lambda e: e.dma_start(out=dbg_d[0], in_=hT[:, 0, :]), reads=[hTB], is_out=True, q="pool")
                P.dma(lambda e: e.dma_start(out=dbg_d[1][:, 0:TOK], in_=ypT[:, 0, :]), reads=[ypB], is_out=True, q="pool")
                P.dma(lambda e: e.dma_start(out=dbg_d[2][:, 0:TOK], in_=ycT[:, 0, :]), reads=[ycB], is_out=True, q="pool")
                P.dma(lambda e: e.dma_start(out=dbg_d[3][:, 0:TOK], in_=ysT[:, 0, :]), reads=[ysB], is_out=True, q="pool")
                P.dma(lambda e: e.dma_start(out=dbg_d[4][:, 0:TOK], in_=mT[:, 0, :]), reads=[mTB], is_out=True, q="pool")
            P.barrier(); ph.close()
            ph = ExitStack()
            sbp = lambda n, s_, d, ph=ph: ph.enter_context(nc.sbuf_tensor(n, s_, d))
            woutb = sbp("woutb" + L, [128, 8, D], BF16); woutB = Buf()
            gpost = sbp("gpost" + L, [128, D], F32); gpostB = Buf()
            ot = [sbp("ot%d" % i + L, [128, 512], F32) for i in range(2)]; otB = [Buf() for _ in range(2)]
            P.dma(lambda e, l=l: e.dma_start(out=woutb[:, :, :], in_=wout_d[l].rearrange("(kc p) n -> p kc n", p=128)), writes=[woutB], q="pool")
            P.dma(lambda e, l=l: e.dma_start(out=gpost[:], in_=gpost_d[l]), writes=[gpostB])
            for tb in range(NTB):
                za, zaB = pj()
                zb, zbB = pj()
                for half, (z, zB) in enumerate(((za, zaB), (zb, zbB))):
                    for kc in range(8):
                        P.op("pe", lambda e, z=z, kc=kc, tb=tb, half=half: e.matmul(z[:], mT[:, kc, tb * 128:(tb + 1) * 128], woutb[:, kc, half * 512:(half + 1) * 512], start=(kc == 0), stop=(kc == 7)),
                             reads=[mTB, woutB], writes=[zB])
                c = 4 * (NTB * l + tb)
                P.op("act", lambda e, za=za, c=c: e.activation(out=KB[6], in_=za[:], func=AF.Square, accum_out=st2[:, c:c + 1]), reads=[zaB], writes=[KBB[6], st2B])
                P.op("act", lambda e, zb=zb, c=c: e.activation(out=KB[6], in_=zb[:], func=AF.Square, accum_out=st2[:, c + 1:c + 2]), reads=[zbB], writes=[KBB[6], st2B])
                P.op("dve", lambda e, c=c: e.tensor_tensor(out=st2[:, c + 2:c + 3], in0=st2[:, c:c + 1], in1=st2[:, c + 1:c + 2], op=ALU.add), reads=[st2B], writes=[st2B])
                P.op("act", lambda e, c=c: e.activation(out=st2[:, c + 3:c + 4], in_=st2[:, c + 2:c + 3], func=AF.Ln, bias=EPS, scale=1.0 / D), reads=[st2B], writes=[st2B])
                P.op("act", lambda e, c=c: e.activation(out=st2[:, c + 2:c + 3], in_=st2[:, c + 3:c + 4], func=AF.Exp, scale=-0.5), reads=[st2B], writes=[st2B])
                oi = tb % 2
                for half, (z, zB) in enumerate(((za, zaB), (zb, zbB))):
                    hs = slice(half * 512, (half + 1) * 512)
                    P.op("dve", lambda e, z=z, c=c, hs=hs, half=half: e.scalar_tensor_tensor(out=ot[half][:], in0=z[:], scalar=st2[:, c + 2:c + 3], in1=gpost[:, hs], op0=ALU.mult, op1=ALU.mult),
                         reads=[zB, st2B, gpostB], writes=[otB[half]])
                    P.op("pool", lambda e, half=half, tb=tb, hs=hs: e.tensor_tensor(out=xres[:, tb, hs], in0=ot[half][:], in1=xres[:, tb, hs], op=ALU.add), reads=[otB[half], xresB[tb]], writes=[xresB[tb]])
                if l == nlayers - 1:
                    P.dma(lambda e, tb=tb: e.dma_start(out=xn_d[tb * 128:(tb + 1) * 128, :], in_=xres[:, tb, :]), reads=[xresB[tb]], is_out=True)
            P.barrier(); ph.close(); ph6.close(); phB.close()
        P.emit()
    return nc


from concourse.bass_utils import run_bass_kernel_spmd

_PROGS = {}


def _consts_A():
    k = np.arange(128)
    ident = np.eye(128, dtype=np.float32)
    negtri = -(k[:, None] >= k[None, :]).astype(np.float32)
    negones = -np.ones((128, 128), np.float32)
    cmask = (k[:, None] < k[None, :]).astype(np.float32)
    return np.ascontiguousarray(np.concatenate([ident, negtri, negones, cmask, cmask], axis=1))


def _pool_mats(p):
    s = np.arange(128)[:, None]
    t = np.arange(128)[None, :]
    out = []
    for w in (2, 4, 8, 16):
        cur = ((s <= t) & (s > t - w)).astype(np.float32) / w - (s == t).astype(np.float32)
        prev = (s > t + 128 - w).astype(np.float32) / w
        if p == 0:
            cnt = np.minimum(t + 1, w).astype(np.float32)
            first = ((s <= t) & (s > t - w)).astype(np.float32) / cnt - (s == t).astype(np.float32)
        else:
            first = cur
        out += [cur, prev, first]
    return np.ascontiguousarray(np.concatenate(out, axis=1).astype(np.float32))


def _run_layer(x, l, pre_norm_g, w_in, pool_w, pool_scale, conv_w, conv_b, w_branch, w_out, post_norm_g):
    if "A" not in _PROGS:
        _PROGS["A"] = build_A()
        _PROGS["B"] = build_B()
    Wl = w_in[l]
    gpre = np.ascontiguousarray(pre_norm_g[l].reshape(8, 128).T)
    cstA = _consts_A()
    mapsA = []
    for c in range(8):
        b, p = c // 2, c % 2
        o = 256 * p
        wq = np.concatenate([Wl[:, 3072 + o:3072 + o + 256], Wl[:, 3584 + o:3584 + o + 256],
                             Wl[:, 4608 + o:4608 + o + 256], Wl[:, 4096 + o:4096 + o + 256]], axis=1)
        mapsA.append({"xf": np.ascontiguousarray(x[b]), "gpre_d": gpre, "wq": np.ascontiguousarray(wq), "cst_d": cstA})
    resA = run_bass_kernel_spmd(_PROGS["A"], mapsA, core_ids=list(range(8))).results
    ys_full = np.zeros((4, 512, S), np.float32)
    for c in range(8):
        b, p = c // 2, c % 2
        y = np.asarray(resA[c]["ysb"])
        for hp in range(2):
            ys_full[b, 256 * p + 128 * hp:256 * p + 128 * hp + 128, :] = y[hp]
    wpool = np.ascontiguousarray(Wl[:, 0:1024])
    poolw = np.ascontiguousarray(np.concatenate([pool_w[l][g] for g in range(4)], axis=1))
    vec = np.zeros((128, 32), np.float32)
    vec[:, 0:4] = pool_scale[l].reshape(4, 128).T
    for k in range(3):
        vec[:, 4 + 4 * k:8 + 4 * k] = conv_w[l][k].reshape(4, 128).T
    vec[:, 16:20] = conv_b[l].reshape(4, 128).T
    wconv = np.stack([np.concatenate([Wl[:, 1024 + 512 * j + 128 * u:1024 + 512 * j + 128 * u + 128] for j in range(4)], axis=1) for u in range(4)])
    wmg = np.stack([np.concatenate([Wl[:, 5120 + 1024 * n + 128 * d:5120 + 1024 * n + 128 * d + 128] for n in range(3)], axis=1) for d in range(8)])
    wbr = np.stack([np.concatenate([w_branch[l][n][:, 128 * d:128 * d + 128] for n in range(3)], axis=1) for d in range(8)])
    gpost = np.ascontiguousarray(np.tile(post_norm_g[l][None, :], (128, 1)))
    ident = np.eye(128, dtype=np.float32)
    mapsB = []
    for c in range(8):
        b, p = c // 2, c % 2
        xo = np.zeros((TOK + 128, D), np.float32)
        if p == 1:
            xo[:] = x[b, TOK - 128:S]
        else:
            xo[128:] = x[b, 0:TOK]
        ysbm = np.ascontiguousarray(ys_full[b][:, p * TOK:(p + 1) * TOK].reshape(4, 128, TOK))
        mapsB.append({"xo": xo, "gpre_d": gpre, "cst_d": ident, "pm_d": _pool_mats(p), "wpool": wpool,
                      "poolw_d": poolw, "vec_d": vec, "wconv": np.ascontiguousarray(wconv), "wmg": np.ascontiguousarray(wmg),
                      "wbr": np.ascontiguousarray(wbr), "wout": np.ascontiguousarray(w_out[l]), "gpost_d": gpost, "ysbm": ysbm})
    resB = run_bass_kernel_spmd(_PROGS["B"], mapsB, core_ids=list(range(8))).results
    xn = np.empty_like(x)
    for c in range(8):
        b, p = c // 2, c % 2
        xn[b, p * TOK:(p + 1) * TOK] = np.asarray(resB[c]["xn"])
    return xn, ys_full


def _fused_inputs(x, pre_norm_g, w_in, pool_w, pool_scale, conv_w, conv_b, w_branch, w_out, post_norm_g):
    gpre = np.stack([pre_norm_g[l].reshape(8, 128).T for l in range(NL)])
    wpool = np.stack([w_in[l][:, 0:1024] for l in range(NL)])
    poolw = np.stack([np.concatenate([pool_w[l][g] for g in range(4)], axis=1) for l in range(NL)])
    vec = np.zeros((NL, 128, 32), np.float32)
    for l in range(NL):
        vec[l, :, 0:4] = pool_scale[l].reshape(4, 128).T
        for k in range(3):
            vec[l, :, 4 + 4 * k:8 + 4 * k] = conv_w[l][k].reshape(4, 128).T
        vec[l, :, 16:20] = conv_b[l].reshape(4, 128).T
    wconv = np.stack([np.stack([np.concatenate([w_in[l][:, 1024 + 512 * j + 128 * u:1024 + 512 * j + 128 * u + 128] for j in range(4)], axis=1) for u in range(4)]) for l in range(NL)])
    wmg = np.stack([np.stack([np.concatenate([w_in[l][:, 5120 + 1024 * n + 128 * d:5120 + 1024 * n + 128 * d + 128] for n in range(3)], axis=1) for d in range(8)]) for l in range(NL)])
    wbr = np.stack([np.stack([np.concatenate([w_branch[l][n][:, 128 * d:128 * d + 128] for n in range(3)], axis=1) for d in range(8)]) for l in range(NL)])
    gpost = np.stack([np.tile(post_norm_g[l][None, :], (128, 1)) for l in range(NL)])
    shared = {"gpre_d": gpre, "cst_d": _consts_A(), "wpool": wpool, "poolw_d": poolw, "vec_d": vec, "wconv": wconv,
              "wmg": wmg, "wbr": wbr, "wout": np.asarray(w_out), "gpost_d": gpost}
    shared = {k: np.ascontiguousarray(v, dtype=np.float32) for k, v in shared.items()}
    maps = []
    for c in range(8):
        b, p = c // 2, c % 2
        o = 256 * p
        wq = np.stack([np.concatenate([w_in[l][:, 3072 + o:3072 + o + 256], w_in[l][:, 3584 + o:3584 + o + 256],
                                       w_in[l][:, 4608 + o:4608 + o + 256], w_in[l][:, 4096 + o:4096 + o + 256]], axis=1) for l in range(NL)])
        flg = np.zeros((128, 4), np.float32)
        flg[:, 0] = float(p == 1)
        flg[:, 1] = float(p == 0)
        flg[:, 2] = float(p == 1)
        m = dict(shared)
        m.update({"xo": np.ascontiguousarray(x[b, p * TOK:(p + 1) * TOK]), "wq": np.ascontiguousarray(wq),
                  "pm_d": _pool_mats(p), "flg_d": flg})
        maps.append(m)
    return maps


def kernel(x, pre_norm_g, w_in, pool_w, pool_scale, conv_w, conv_b, w_branch, w_out, post_norm_g):
    args = [np.asarray(a, dtype=np.float32) for a in (pre_norm_g, w_in, pool_w, pool_scale, conv_w, conv_b, w_branch, w_out, post_norm_g)]
    x = np.asarray(x, dtype=np.float32)
    if "F" not in _PROGS:
        _PROGS["F"] = build_fused()
    maps = _fused_inputs(x, *args)
    res = run_bass_kernel_spmd(_PROGS["F"], maps, core_ids=list(range(8))).results
    out = np.empty_like(x)
    for c in range(8):
        b, p = c // 2, c % 2
        out[b, p * TOK:(p + 1) * TOK] = np.asarray(res[c]["xn"])
    return out
```

```python
import numpy as np
import concourse.bass as bass
import concourse.mybir as mybir
from contextlib import ExitStack

F32 = mybir.dt.float32
BF16 = mybir.dt.bfloat16
AF = mybir.ActivationFunctionType
ALU = mybir.AluOpType
AX = mybir.AxisListType

CENG = ["pe", "act", "dve", "pool"]
NDMASEM = 24


import types as _types


def _freeze(fn):
    if fn is None or fn.__closure__ is None:
        return fn
    cells = []
    for c in fn.__closure__:
        try:
            cells.append(_types.CellType(c.cell_contents))
        except ValueError:
            cells.append(c)
    return _types.FunctionType(fn.__code__, fn.__globals__, fn.__name__, fn.__defaults__, tuple(cells))


class Buf:
    __slots__ = ("name", "w", "r")

    def __init__(self, name=""):
        self.name = name
        self.w = None
        self.r = []


class Op:
    __slots__ = ("eng", "fn", "deps", "kind", "sig", "semi", "semval", "idx")

    def __init__(self, eng, fn, kind):
        self.eng = eng
        self.fn = fn
        self.deps = []
        self.kind = kind
        self.sig = False
        self.semi = None
        self.semval = None
        self.idx = None


class Prog:
    def __init__(self, nc):
        self.nc = nc
        self.ops = []
        self.per_eng = {e: [] for e in CENG + ["sp"]}
        self.dma_rr = 0
        self.dma_rr_k = {}
        self.dma_last = [None] * NDMASEM
        self.dma_cnt = [0] * NDMASEM
        self.out_dmas = []

    def _track(self, op, reads, writes):
        deps = []
        for b in reads:
            if b.w is not None:
                deps.append(b.w)
        for b in writes:
            if b.w is not None:
                deps.append(b.w)
            deps.extend(b.r)
        for b in reads:
            b.r.append(op)
        for b in writes:
            b.w = op
            b.r = []
        seen = set()
        for d in deps:
            if d is op or id(d) in seen:
                continue
            if d.kind == "c" and op.kind == "c" and d.eng == "pe" and op.eng == "pe":
                continue
            seen.add(id(d))
            op.deps.append(d)
            d.sig = True

    def op(self, eng, fn, reads=(), writes=()):
        o = Op(eng, _freeze(fn), "c")
        self._track(o, reads, writes)
        self.ops.append(o)
        self.per_eng[eng].append(o)
        return o

    def dma(self, fn, reads=(), writes=(), q="sp", is_out=False, inc=16):
        o = Op(q, _freeze(fn), "d")
        o.idx = inc
        if inc == 1:
            lo, n = 22, 2
        elif q == "pool":
            lo, n = 14, 8
        else:
            lo, n = 0, 14
        k = self.dma_rr_k.get(lo, 0)
        self.dma_rr_k[lo] = (k + 1) % n
        s = lo + k
        prev = self.dma_last[s]
        self._track(o, reads, writes)
        if prev is not None and prev not in o.deps:
            o.deps.append(prev)
        self.dma_cnt[s] += inc
        o.semi = s
        o.semval = self.dma_cnt[s]
        o.sig = True
        self.dma_last[s] = o
        self.ops.append(o)
        self.per_eng[q].append(o)
        if is_out:
            self.out_dmas.append(o)
        return o

    def barrier(self):
        lasts = []
        for e in CENG:
            if self.per_eng[e]:
                for o in reversed(self.per_eng[e]):
                    if o.kind == "c" and o.fn is not None:
                        lasts.append(o)
                        break
        for s in range(NDMASEM):
            if self.dma_last[s] is not None and self.dma_last[s].idx != 1:
                lasts.append(self.dma_last[s])
        for e in CENG + ["sp"]:
            o = Op(e, None, "c")
            for d in lasts:
                if d.kind == "c" and d.eng == e and e == "pe":
                    continue
                o.deps.append(d)
                d.sig = True
            self.ops.append(o)
            self.per_eng[e].append(o)

    def emit(self):
        nc = self.nc
        with ExitStack() as st:
            esem = {e: st.enter_context(nc.semaphore("s_" + e)) for e in CENG}
            dsem = [st.enter_context(nc.semaphore("d_%d" % i)) for i in range(NDMASEM)]
            block = st.enter_context(nc.Block())
            cnt = {e: 0 for e in CENG}
            for o in self.ops:
                if o.kind == "c" and o.fn is not None:
                    if o.sig:
                        cnt[o.eng] += 1
                        o.semval = cnt[o.eng]
            fin = Op("sp", None, "c")
            fin.deps = list(self.out_dmas)
            self.per_eng["sp"].append(fin)

            def run(engname, eng):
                waited = {}
                for o in self.per_eng[engname]:
                    need = {}
                    for d in o.deps:
                        if d.kind == "c":
                            if d.fn is None:
                                continue
                            key = ("e", d.eng)
                        else:
                            key = ("d", d.semi)
                        if d.semval > need.get(key, 0):
                            need[key] = d.semval
                    for key, v in need.items():
                        if waited.get(key, 0) >= v:
                            continue
                        waited[key] = v
                        sem = esem[key[1]] if key[0] == "e" else dsem[key[1]]
                        eng.wait_ge(sem, v)
                    if o.fn is None:
                        continue
                    inst = o.fn(eng)
                    if o.kind == "d":
                        inst.then_inc(dsem[o.semi], o.idx)
                    elif o.sig:
                        inst.then_inc(esem[o.eng], 1)

            @block.tensor
            def _(e):
                run("pe", e)

            @block.scalar
            def _(e):
                run("act", e)

            @block.vector
            def _(e):
                run("dve", e)

            @block.gpsimd
            def _(e):
                run("pool", e)

            @block.sync
            def _(e):
                run("sp", e)


S = 4096
D = 1024
TOK = 2048
NTB = 16
EPS = 1e-6


def _alloc(nc, st):
    sb = lambda n, s, d: st.enter_context(nc.sbuf_tensor(n, s, d))
    ps = lambda n, s, d: st.enter_context(nc.psum_tensor(n, s, d))
    return sb, ps


def emit_norm_block(P, nc, T, B, x_src_fn, tb_idx, hT, hT_buf, col0, gpre, slot):
    xt, xtB = T["xt"][slot], B["xt"][slot]
    hb, hbB = T["hb"][slot], B["hb"][slot]
    st, stB = T["st"], B["st"]
    P.dma(lambda e: e.dma_start(out=xt[:], in_=x_src_fn()), writes=[xtB])
    c = tb_idx
    P.op("act", lambda e: e.activation(out=T["junk"][:], in_=xt[:], func=AF.Square, accum_out=st[:, 3 * c:3 * c + 1]),
         reads=[xtB], writes=[B["junk"], stB])
    P.op("act", lambda e: e.activation(out=st[:, 3 * c + 1:3 * c + 2], in_=st[:, 3 * c:3 * c + 1], func=AF.Ln, bias=EPS, scale=1.0 / D),
         reads=[stB], writes=[stB])
    P.op("act", lambda e: e.activation(out=st[:, 3 * c + 2:3 * c + 3], in_=st[:, 3 * c + 1:3 * c + 2], func=AF.Exp, scale=-0.5),
         reads=[stB], writes=[stB])
    P.op("dve", lambda e: e.tensor_scalar_mul(out=hb[:], in0=xt[:], scalar1=st[:, 3 * c + 2:3 * c + 3]),
         reads=[xtB, stB], writes=[hbB])
    tp, tpB = T["tp"], B["tp"]
    for dc in range(8):
        P.op("pe", lambda e, dc=dc: e.transpose(tp[:, dc, :], hb[:, dc * 128:(dc + 1) * 128], T["ident"][:]),
             reads=[hbB, B["const"]], writes=[tpB])
    for dc in range(8):
        eng = "dve" if dc % 2 == 0 else "pool"
        if eng == "pool":
            eng = "dve"
        P.op(eng, lambda e, dc=dc: e.tensor_scalar_mul(out=hT[:, dc, col0:col0 + 128], in0=tp[:, dc, :], scalar1=gpre[:, dc:dc + 1]),
             reads=[tpB, B["const"]], writes=[hT_buf])


def build_A(dbg=False):
    nc = bass.Bass("TRN2", target_bir_lowering=False)
    dt_in = lambda n, s: nc.dram_tensor(n, s, F32, kind="ExternalInput").ap()
    xf = dt_in("xf", [S, D])
    gpre_d = dt_in("gpre_d", [128, 8])
    wq_d = dt_in("wq", [D, 1024])
    cst_d = dt_in("cst_d", [128, 4 * 128])
    ysb_d = nc.dram_tensor("ysb", [2, 128, S], F32, kind="ExternalOutput").ap()
    P = Prog(nc)
    with ExitStack() as st_:
        sb, ps = _alloc(nc, st_)
        T, B = {}, {}
        T["xt"] = [sb("xt%d" % i, [128, D], F32) for i in range(2)]
        B["xt"] = [Buf() for i in range(2)]
        T["hb"] = [sb("hb%d" % i, [128, D], BF16) for i in range(2)]
        B["hb"] = [Buf() for i in range(2)]
        T["junk"] = sb("junk", [128, D], BF16); B["junk"] = Buf()
        T["st"] = sb("st", [128, 3 * 32], F32); B["st"] = Buf()
        cst = sb("cst", [128, 512], BF16); B["const"] = Buf()
        T["ident"] = cst[:, 0:128]
        negtri = cst[:, 128:256]; negones = cst[:, 256:384]; cmask = cst[:, 384:512]
        gpre = sb("gpre", [128, 8], F32)
        hT = sb("hT", [128, 8, S], BF16); hTB = Buf()
        wqb = sb("wqb", [128, 8, 1024], BF16); wqB = Buf()
        qT = sb("qT", [128, 2, S], BF16); kT = sb("kT", [128, 2, S], BF16)
        sgT = sb("sgT", [128, 2, S], BF16); vt = sb("vt", [128, 32, 256], BF16)
        qB, kB, sgB, vB = Buf(), Buf(), Buf(), Buf()
        ysb = sb("ysbT", [128, 2, S], BF16); ysbB = Buf()
        ee = [sb("ee%d" % i, [128, 512], F32) for i in range(2)]; eeB = [Buf() for _ in range(2)]
        spp = [sb("spp%d" % i, [128, 512], BF16) for i in range(3)]; sppB = [Buf() for _ in range(3)]
        AT = [sb("AT%d" % i, [128, 512], BF16) for i in range(2)]; ATB = [Buf() for _ in range(2)]
        Sf = [sb("Sf%d" % i, [128, 512], F32) for i in range(2)]; SfB = [Buf() for _ in range(2)]
        Sb = [[sb("Sb%d_%d" % (i, j), [128, 512], BF16) for j in range(2)] for i in range(2)]
        SbB = [[Buf() for j in range(2)] for i in range(2)]
        T["tp"] = ps("tp", [128, 8, 128], BF16); B["tp"] = Buf()
        Z1 = [ps("Z1%d" % i, [128, 512], F32) for i in range(2)]; Z1B = [Buf() for _ in range(2)]
        Z2 = [ps("Z2%d" % i, [128, 512], F32) for i in range(2)]; Z2B = [Buf() for _ in range(2)]
        OO = [ps("OO%d" % i, [128, 512], F32) for i in range(2)]; OOB = [Buf() for _ in range(2)]

        P.dma(lambda e: e.dma_start(out=cst[:], in_=cst_d), writes=[B["const"]], q="pool")
        P.dma(lambda e: e.dma_start(out=gpre[:], in_=gpre_d), writes=[B["const"]])
        P.dma(lambda e: e.dma_start(out=wqb[:], in_=wq_d.rearrange("(kc p) n -> p kc n", p=128)), writes=[wqB], q="pool")
        for tb in range(32):
            emit_norm_block(P, nc, T, B, (lambda tb=tb: xf[tb * 128:(tb + 1) * 128, :]), tb, hT, hTB, tb * 128, gpre, tb % 2)
        k = 0
        for cc in range(6):
            for tc in range(8):
                zi = k % 2; k += 1
                for kc in range(8):
                    P.op("pe", lambda e, zi=zi, kc=kc, cc=cc, tc=tc: e.matmul(Z1[zi][:], wqb[:, kc, cc * 128:(cc + 1) * 128], hT[:, kc, tc * 512:(tc + 1) * 512], start=(kc == 0), stop=(kc == 7)),
                         reads=[wqB, hTB], writes=[Z1B[zi]])
                sl = slice(tc * 512, (tc + 1) * 512)
                if cc < 2:
                    P.op("dve", lambda e, zi=zi, cc=cc, sl=sl: e.tensor_scalar_mul(out=qT[:, cc, sl], in0=Z1[zi][:], scalar1=0.125), reads=[Z1B[zi]], writes=[qB])
                elif cc < 4:
                    P.op("dve", lambda e, zi=zi, cc=cc, sl=sl: e.tensor_copy(out=kT[:, cc - 2, sl], in_=Z1[zi][:]), reads=[Z1B[zi]], writes=[kB])
                else:
                    P.op("act", lambda e, zi=zi, cc=cc, sl=sl: e.activation(out=sgT[:, cc - 4, sl], in_=Z1[zi][:], func=AF.Silu), reads=[Z1B[zi]], writes=[sgB])
        for tb in range(32):
            zi = k % 2; k += 1
            for kc in range(8):
                P.op("pe", lambda e, zi=zi, kc=kc, tb=tb: e.matmul(Z1[zi][:, 0:256], hT[:, kc, tb * 128:(tb + 1) * 128], wqb[:, kc, 768:1024], start=(kc == 0), stop=(kc == 7)),
                     reads=[wqB, hTB], writes=[Z1B[zi]])
            P.op("dve", lambda e, zi=zi, tb=tb: e.tensor_copy(out=vt[:, tb, :], in_=Z1[zi][:, 0:256]), reads=[Z1B[zi]], writes=[vB])

        tiles = []
        for hp in range(2):
            for G in range(8):
                for kb in range(4 * G + 3, -1, -1):
                    for hh in range(2):
                        tiles.append((hp, G, kb, hh))
        nS = {}

        def stage1(t, tl):
            hp, G, kb, hh = tl
            r = kb - 4 * G if kb >= 4 * G else 0
            c0 = 128 * r
            zi, ei, si = t % 2, t % 2, t % 3
            ps_ = slice(hh * 64, (hh + 1) * 64)
            P.op("pe", lambda e: e.matmul(Z1[zi][:, c0:512], kT[ps_, hp, kb * 128:(kb + 1) * 128], qT[ps_, hp, G * 512 + c0:(G + 1) * 512], start=True, stop=True),
                 reads=[kB, qB], writes=[Z1B[zi]])
            P.op("act", lambda e: e.activation(out=ee[ei][:, c0:512], in_=Z1[zi][:, c0:512], func=AF.Exp), reads=[Z1B[zi]], writes=[eeB[ei]])
            P.op("act", lambda e: e.activation(out=spp[si][:, c0:512], in_=ee[ei][:, c0:512], func=AF.Ln, bias=1.0, scale=1.0), reads=[eeB[ei]], writes=[sppB[si]])
            if kb >= 4 * G:
                P.op("pool", lambda e: e.tensor_tensor(out=spp[si][:, c0:c0 + 128], in0=spp[si][:, c0:c0 + 128], in1=cmask, op=ALU.mult),
                     reads=[sppB[si], B["const"]], writes=[sppB[si]])

        def stage2(t, tl):
            hp, G, kb, hh = tl
            diag = kb >= 4 * G
            r = kb - 4 * G if diag else 0
            c0 = 128 * r
            zi, si, ai = t % 2, t % 3, t % 2
            s = hh
            top = (kb == 4 * G + 3)
            n = nS.get((hp, G, hh), 0)
            nS[(hp, G, hh)] = n + 1
            ps_ = slice(hh * 64, (hh + 1) * 64)
            if top:
                P.op("dve", lambda e: e.memset(Sf[s][:], 0.0), writes=[SfB[s]])
            P.op("pe", lambda e: e.matmul(Z2[zi][:, c0:512], kT[ps_, hp, kb * 128:(kb + 1) * 128], qT[ps_, hp, G * 512 + c0:(G + 1) * 512], start=True, stop=False),
                 reads=[kB, qB], writes=[Z2B[zi]])
            P.op("pe", lambda e: e.matmul(Z2[zi][:, c0:512], negtri, spp[si][:, c0:512], start=False, stop=top),
                 reads=[sppB[si], B["const"]], writes=[Z2B[zi]])
            if not top:
                pj = (n - 1) % 2
                P.op("pe", lambda e: e.matmul(Z2[zi][:, c0:512], negones, Sb[s][pj][:, c0:512], start=False, stop=True),
                     reads=[SbB[s][pj], B["const"]], writes=[Z2B[zi]])
            if kb > 0:
                P.op("dve", lambda e: e.tensor_tensor(out=Sf[s][:, c0:512], in0=Sf[s][:, c0:512], in1=spp[si][:, c0:512], op=ALU.add),
                     reads=[SfB[s], sppB[si]], writes=[SfB[s]])
                P.op("dve", lambda e: e.tensor_copy(out=Sb[s][n % 2][:], in_=Sf[s][:]), reads=[SfB[s]], writes=[SbB[s][n % 2]])
            P.op("act", lambda e: e.activation(out=AT[ai][:, c0:512], in_=Z2[zi][:, c0:512], func=AF.Exp), reads=[Z2B[zi]], writes=[ATB[ai]])
            if diag:
                P.op("pool", lambda e: e.tensor_tensor(out=AT[ai][:, c0:c0 + 128], in0=AT[ai][:, c0:c0 + 128], in1=cmask, op=ALU.mult),
                     reads=[ATB[ai], B["const"]], writes=[ATB[ai]])
            vsl = vt[:, kb, hp * 128:(hp + 1) * 128]
            last = (kb == 0)
            o0 = 0 if top else c0
            if top:
                P.op("pool", lambda e: e.memset(AT[ai][:, 0:c0], 0.0), writes=[ATB[ai]])
            P.op("pe", lambda e: e.matmul(OO[s][:, o0:512], vsl, AT[ai][:, o0:512], start=top, stop=last),
                 reads=[vB, ATB[ai]], writes=[OOB[s]])
            if last:
                gs = slice(G * 512, (G + 1) * 512)
                P.op("dve", lambda e: e.tensor_tensor(out=ysb[ps_, hp, gs], in0=OO[s][ps_, :], in1=sgT[ps_, hp, gs], op=ALU.mult),
                     reads=[OOB[s], sgB], writes=[ysbB])

        if dbg:
            tiles = [tl for tl in tiles if tl[0] == 0 and tl[1] == 0]
        stage1(0, tiles[0])
        for t in range(len(tiles)):
            if t + 1 < len(tiles):
                stage1(t + 1, tiles[t + 1])
            stage2(t, tiles[t])
        if dbg:
            dbg_d = nc.dram_tensor("dbg", [128, 5 * 512], F32, kind="ExternalOutput").ap()
            dsb = sb("dsb", [128, 2 * 512], F32); dsbB = Buf()
            P.op("act", lambda e: e.activation(out=dsb[:, 0:512], in_=OO[0][:], func=AF.Copy), reads=[OOB[0]], writes=[dsbB])
            P.op("act", lambda e: e.activation(out=dsb[:, 512:1024], in_=OO[1][:], func=AF.Copy), reads=[OOB[1]], writes=[dsbB])
            P.dma(lambda e: e.dma_start(out=dbg_d[:, 0:512], in_=Sf[0][:]), reads=[SfB[0]], is_out=True)
            P.dma(lambda e: e.dma_start(out=dbg_d[:, 512:1024], in_=Sf[1][:]), reads=[SfB[1]], is_out=True)
            P.dma(lambda e: e.dma_start(out=dbg_d[:, 1024:2048], in_=dsb[:]), reads=[dsbB], is_out=True)
            P.dma(lambda e: e.dma_start(out=dbg_d[:, 2048:2560], in_=Sb[0][0][:]), reads=[SbB[0][0]], is_out=True, q="pool")
        for hp in range(2):
            P.dma(lambda e, hp=hp: e.dma_start(out=ysb_d[hp], in_=ysb[:, hp, :]), reads=[ysbB], is_out=True, q="pool")
        P.emit()
    return nc


def build_B():
    nc = bass.Bass("TRN2", target_bir_lowering=False)
    dt_in = lambda n, s: nc.dram_tensor(n, s, F32, kind="ExternalInput").ap()
    xo = dt_in("xo", [TOK + 128, D])
    gpre_d = dt_in("gpre_d", [128, 8])
    cst_d = dt_in("cst_d", [128, 128])
    pm_d = dt_in("pm_d", [128, 12 * 128])
    wpool_d = dt_in("wpool", [D, 1024])
    poolw_d = dt_in("poolw_d", [128, 4 * 128])
    vec_d = dt_in("vec_d", [128, 32])
    wconv_d = dt_in("wconv", [4, D, 512])
    wmg_d = dt_in("wmg", [8, D, 384])
    wbr_d = dt_in("wbr", [8, 512, 3 * 384 // 3 * 1])
    wout_d = dt_in("wout", [D, D])
    gpost_d = dt_in("gpost_d", [128, D])
    ysb_d = dt_in("ysbm", [4, 128, TOK])
    xn_d = nc.dram_tensor("xn", [TOK, D], F32, kind="ExternalOutput").ap()
    P = Prog(nc)
    NT1 = TOK + 128
    with ExitStack() as st_:
        sb, ps = _alloc(nc, st_)
        T, B = {}, {}
        T["xt"] = [sb("xt%d" % i, [128, D], F32) for i in range(2)]
        B["xt"] = [Buf() for i in range(2)]
        T["hb"] = [sb("hb%d" % i, [128, D], BF16) for i in range(2)]
        B["hb"] = [Buf() for i in range(2)]
        T["junk"] = sb("junk", [128, D], BF16); B["junk"] = Buf()
        T["st"] = sb("st", [128, 3 * 17], F32); B["st"] = Buf()
        cst = sb("cst", [128, 128], BF16); B["const"] = Buf()
        T["ident"] = cst[:, 0:128]
        pm = sb("pm", [128, 12 * 128], BF16)
        poolw = sb("poolw", [128, 512], BF16)
        vec = sb("vec", [128, 32], F32)
        gpre = sb("gpre", [128, 8], F32)
        gpost = sb("gpost", [128, D], F32)
        hT = sb("hT", [128, 8, NT1], BF16); hTB = Buf()
        ypT = sb("ypT", [128, 4, TOK], BF16); ypB = Buf()
        ycT = sb("ycT", [128, 4, TOK], BF16); ycB = Buf()
        ysT = sb("ysT", [128, 4, TOK], BF16); ysB = Buf()
        wch = [sb("wch%d" % i, [128, 8, 512], BF16) for i in range(2)]; wchB = [Buf() for _ in range(2)]
        ph = ExitStack()
        sb = lambda n, s_, d: ph.enter_context(nc.sbuf_tensor(n, s_, d))
        pvt = sb("pvt", [128, 17, 512], BF16); pvtB = Buf()
        spg = sb("spg", [128, 4, TOK], BF16); spgB = Buf()
        plb = [sb("plb%d" % i, [128, 512], BF16) for i in range(2)]; plbB = [Buf() for _ in range(2)]
        T["tp"] = ps("tp", [128, 8, 128], BF16); B["tp"] = Buf()
        PJ = [ps("PJ%d" % i, [128, 512], F32) for i in range(6)]; PJB = [Buf() for _ in range(6)]
        pjk = [0]

        def pj():
            i = pjk[0] % 6
            pjk[0] += 1
            return PJ[i], PJB[i]

        P.dma(lambda e: e.dma_start(out=cst[:], in_=cst_d), writes=[B["const"]], q="pool")
        P.dma(lambda e: e.dma_start(out=pm[:], in_=pm_d), writes=[B["const"]], q="pool")
        P.dma(lambda e: e.dma_start(out=poolw[:], in_=poolw_d), writes=[B["const"]], q="pool")
        P.dma(lambda e: e.dma_start(out=vec[:], in_=vec_d), writes=[B["const"]])
        P.dma(lambda e: e.dma_start(out=gpre[:], in_=gpre_d), writes=[B["const"]])
        P.dma(lambda e: e.dma_start(out=gpost[:], in_=gpost_d), writes=[B["const"]])
        P.dma(lambda e: e.dma_start(out=wch[0][:, :, :], in_=wpool_d[:, 0:512].rearrange("(kc p) n -> p kc n", p=128)), writes=[wchB[0]], q="pool")
        P.dma(lambda e: e.dma_start(out=wch[1][:, :, :], in_=wpool_d[:, 512:1024].rearrange("(kc p) n -> p kc n", p=128)), writes=[wchB[1]], q="pool")
        for i in range(4):
            P.dma(lambda e, i=i: e.dma_start(out=ysT[:, i, :], in_=ysb_d[i]), writes=[ysB], q="pool")
        for tb in range(17):
            emit_norm_block(P, nc, T, B, (lambda tb=tb: xo[tb * 128:(tb + 1) * 128, :]), tb, hT, hTB, tb * 128, gpre, tb % 2)
        for blk in range(17):
            z, zB = pj()
            for kc in range(8):
                P.op("pe", lambda e, z=z, kc=kc, blk=blk: e.matmul(z[:], hT[:, kc, blk * 128:(blk + 1) * 128], wch[0][:, kc, :], start=(kc == 0), stop=(kc == 7)),
                     reads=[hTB, wchB[0]], writes=[zB])
            P.op("dve", lambda e, z=z, blk=blk: e.tensor_copy(out=pvt[:, blk, :], in_=z[:]), reads=[zB], writes=[pvtB])
        for cc in range(4):
            for tc in range(4):
                z, zB = pj()
                for kc in range(8):
                    P.op("pe", lambda e, z=z, kc=kc, cc=cc, tc=tc: e.matmul(z[:], wch[1][:, kc, cc * 128:(cc + 1) * 128], hT[:, kc, 128 + tc * 512:128 + (tc + 1) * 512], start=(kc == 0), stop=(kc == 7)),
                         reads=[hTB, wchB[1]], writes=[zB])
                P.op("act", lambda e, z=z, cc=cc, tc=tc: e.activation(out=spg[:, cc, tc * 512:(tc + 1) * 512], in_=z[:], func=AF.Silu), reads=[zB], writes=[spgB])
        def load_conv(u):
            P.dma(lambda e: e.dma_start(out=wch[u % 2][:, :, :], in_=wconv_d[u].rearrange("(kc p) n -> p kc n", p=128)), writes=[wchB[u % 2]], q="pool")
        kk = 0
        for g in range(4):
            for tc in range(4):
                z, zB = pj()
                for j in range(4):
                    b = 1 + 4 * tc + j
                    first = (b == 1)
                    pmc = pm[:, (3 * g + (2 if first else 0)) * 128:(3 * g + (2 if first else 0) + 1) * 128]
                    pmp = pm[:, (3 * g + 1) * 128:(3 * g + 2) * 128]
                    P.op("pe", lambda e, z=z, j=j, b=b, g=g, pmc=pmc: e.matmul(z[:, j * 128:(j + 1) * 128], pvt[:, b, g * 128:(g + 1) * 128], pmc, start=True, stop=False),
                         reads=[pvtB, B["const"]], writes=[zB])
                    P.op("pe", lambda e, z=z, j=j, b=b, g=g, pmp=pmp: e.matmul(z[:, j * 128:(j + 1) * 128], pvt[:, b - 1, g * 128:(g + 1) * 128], pmp, start=False, stop=True),
                         reads=[pvtB, B["const"]], writes=[zB])
                pi = kk % 2; kk += 1
                P.op("act", lambda e, z=z, pi=pi: e.activation(out=plb[pi][:], in_=z[:], func=AF.Copy), reads=[zB], writes=[plbB[pi]])
                z2, z2B = pj()
                P.op("pe", lambda e, z2=z2, pi=pi, g=g: e.matmul(z2[:], poolw[:, g * 128:(g + 1) * 128], plb[pi][:], start=True, stop=True),
                     reads=[plbB[pi], B["const"]], writes=[z2B])
                sl = slice(tc * 512, (tc + 1) * 512)
                P.op("dve", lambda e, z2=z2, g=g, sl=sl: e.scalar_tensor_tensor(out=ypT[:, g, sl], in0=z2[:], scalar=vec[:, g:g + 1], in1=spg[:, g, sl], op0=ALU.mult, op1=ALU.mult),
                     reads=[z2B, spgB, B["const"]], writes=[ypB])
        P.barrier(); ph.close(); ph = ExitStack()
        sb = lambda n, s_, d: ph.enter_context(nc.sbuf_tensor(n, s_, d))
        cxT = sb("cxT", [128, NT1], F32); gcT = sb("gcT", [128, NT1], F32)
        gbT = sb("gbT", [128, TOK], F32); scg = sb("scg", [128, TOK], F32)
        zz = sb("zz", [128, NT1], F32); acc = sb("acc", [128, TOK], F32)
        cxB, gcB, gbB, scgB, zzB, accB = Buf(), Buf(), Buf(), Buf(), Buf(), Buf()
        load_conv(0)
        for u in range(4):
            if u + 1 < 4:
                load_conv(u + 1)
            w_, wB_ = wch[u % 2], wchB[u % 2]
            for which, dst, dB, c_off in ((0, cxT, cxB, 0), (2, gcT, gcB, 256)):
                for tcx in range(5):
                    t0, t1 = (0, 128) if tcx == 0 else (128 + (tcx - 1) * 512, 128 + tcx * 512)
                    z, zB = pj()
                    for kc in range(8):
                        P.op("pe", lambda e, z=z, kc=kc, t0=t0, t1=t1, c_off=c_off, w_=w_: e.matmul(z[:, 0:t1 - t0], w_[:, kc, c_off:c_off + 128], hT[:, kc, t0:t1], start=(kc == 0), stop=(kc == 7)),
                             reads=[hTB, wB_], writes=[zB])
                    P.op("act", lambda e, z=z, t0=t0, t1=t1, dst=dst: e.activation(out=dst[:, t0:t1], in_=z[:, 0:t1 - t0], func=AF.Copy), reads=[zB], writes=[dB])
            for c_off, dst, dB, fn in ((128, gbT, gbB, AF.Copy), (384, scg, scgB, AF.Silu)):
                for tc in range(4):
                    z, zB = pj()
                    for kc in range(8):
                        P.op("pe", lambda e, z=z, kc=kc, tc=tc, c_off=c_off, w_=w_: e.matmul(z[:], w_[:, kc, c_off:c_off + 128], hT[:, kc, 128 + tc * 512:128 + (tc + 1) * 512], start=(kc == 0), stop=(kc == 7)),
                             reads=[hTB, wB_], writes=[zB])
                    P.op("act", lambda e, z=z, tc=tc, dst=dst, fn=fn: e.activation(out=dst[:, tc * 512:(tc + 1) * 512], in_=z[:], func=fn), reads=[zB], writes=[dB])
            w0 = vec[:, 4 + u:5 + u]; w1 = vec[:, 8 + u:9 + u]; w2 = vec[:, 12 + u:13 + u]; cb = vec[:, 16 + u:17 + u]
            P.op("pool", lambda e: e.tensor_tensor(out=zz[:], in0=gcT[:], in1=cxT[:], op=ALU.mult), reads=[gcB, cxB], writes=[zzB])
            P.op("dve", lambda e, w2=w2, cb=cb: e.tensor_scalar(out=acc[:], in0=zz[:, 128:NT1], scalar1=w2, scalar2=cb, op0=ALU.mult, op1=ALU.add), reads=[zzB, B["const"]], writes=[accB])
            P.op("dve", lambda e, w1=w1: e.scalar_tensor_tensor(out=acc[:], in0=zz[:, 127:NT1 - 1], scalar=w1, in1=acc[:], op0=ALU.mult, op1=ALU.add), reads=[zzB, accB, B["const"]], writes=[accB])
            P.op("dve", lambda e, w0=w0: e.scalar_tensor_tensor(out=acc[:], in0=zz[:, 126:NT1 - 2], scalar=w0, in1=acc[:], op0=ALU.mult, op1=ALU.add), reads=[zzB, accB, B["const"]], writes=[accB])
            P.op("pool", lambda e: e.tensor_tensor(out=acc[:], in0=acc[:], in1=gbT[:], op=ALU.mult), reads=[accB, gbB], writes=[accB])
            P.op("dve", lambda e, u=u: e.tensor_tensor(out=ycT[:, u, :], in0=acc[:], in1=scg[:], op=ALU.mult), reads=[accB, scgB], writes=[ycB])
        P.barrier(); ph.close(); ph = ExitStack()
        sb = lambda n, s_, d: ph.enter_context(nc.sbuf_tensor(n, s_, d))
        mT = sb("mT", [128, 8, TOK], BF16); mTB = Buf()
        wmg = [sb("wmg%d" % i, [128, 8, 384], BF16) for i in range(2)]; wmgB = [Buf() for _ in range(2)]
        wbr = [sb("wbr%d" % i, [128, 4, 384], BF16) for i in range(2)]; wbrB = [Buf() for _ in range(2)]
        gsb = [sb("gsb%d" % i, [128, 512], F32) for i in range(2)]; gsbB = [Buf() for _ in range(2)]
        mac = [sb("mac%d" % i, [128, 512], F32) for i in range(2)]; macB = [Buf() for _ in range(2)]
        tmpb = [sb("tmpb%d" % i, [128, 512], F32) for i in range(2)]; tmpB = [Buf() for _ in range(2)]
        woutb = sb("woutb", [128, 8, D], BF16); woutB = Buf()

        def load_m(dcn):
            P.dma(lambda e: e.dma_start(out=wmg[dcn % 2][:, :, :], in_=wmg_d[dcn].rearrange("(kc p) n -> p kc n", p=128)), writes=[wmgB[dcn % 2]], q="pool")
            P.dma(lambda e: e.dma_start(out=wbr[dcn % 2][:, :, :], in_=wbr_d[dcn].rearrange("(kc p) n -> p kc n", p=128)), writes=[wbrB[dcn % 2]], q="pool")
        load_m(0)
        P.dma(lambda e: e.dma_start(out=woutb[:, :, :], in_=wout_d.rearrange("(kc p) n -> p kc n", p=128)), writes=[woutB], q="pool")
        yTs = [(ypT, ypB), (ycT, ycB), (ysT, ysB)]
        gk = 0
        for dcn in range(8):
            if dcn + 1 < 8:
                load_m(dcn + 1)
            wm_, wmB_ = wmg[dcn % 2], wmgB[dcn % 2]
            wb_, wbB_ = wbr[dcn % 2], wbrB[dcn % 2]
            for tc in range(4):
                sl = slice(tc * 512, (tc + 1) * 512)
                mi = (dcn * 4 + tc) % 2
                for n in range(3):
                    zg, zgB = pj()
                    for kc in range(8):
                        P.op("pe", lambda e, zg=zg, kc=kc, n=n, tc=tc, wm_=wm_: e.matmul(zg[:], wm_[:, kc, n * 128:(n + 1) * 128], hT[:, kc, 128 + tc * 512:128 + (tc + 1) * 512], start=(kc == 0), stop=(kc == 7)),
                             reads=[hTB, wmB_], writes=[zgB])
                    gi = gk % 2; gk += 1
                    P.op("act", lambda e, zg=zg, gi=gi: e.activation(out=gsb[gi][:], in_=zg[:], func=AF.Sigmoid), reads=[zgB], writes=[gsbB[gi]])
                    zp, zpB = pj()
                    yT_, yB_ = yTs[n]
                    for wc in range(4):
                        P.op("pe", lambda e, zp=zp, wc=wc, n=n, sl=sl, wb_=wb_, yT_=yT_: e.matmul(zp[:], wb_[:, wc, n * 128:(n + 1) * 128], yT_[:, wc, sl], start=(wc == 0), stop=(wc == 3)),
                             reads=[yB_, wbB_], writes=[zpB])
                    if n == 0:
                        P.op("dve", lambda e, zp=zp, gi=gi, mi=mi: e.tensor_tensor(out=mac[mi][:], in0=zp[:], in1=gsb[gi][:], op=ALU.mult), reads=[zpB, gsbB[gi]], writes=[macB[mi]])
                    else:
                        P.op("dve", lambda e, zp=zp, gi=gi, mi=mi: e.tensor_tensor(out=tmpb[mi][:], in0=zp[:], in1=gsb[gi][:], op=ALU.mult), reads=[zpB, gsbB[gi]], writes=[tmpB[mi]])
                        if n == 1:
                            P.op("pool", lambda e, mi=mi: e.tensor_tensor(out=mac[mi][:], in0=mac[mi][:], in1=tmpb[mi][:], op=ALU.add), reads=[macB[mi], tmpB[mi]], writes=[macB[mi]])
                        else:
                            P.op("pool", lambda e, mi=mi, dcn=dcn, sl=sl: e.tensor_tensor(out=mT[:, dcn, sl], in0=mac[mi][:], in1=tmpb[mi][:], op=ALU.add), reads=[macB[mi], tmpB[mi]], writes=[mTB])
        st2 = sb("st2", [128, 4 * NTB], F32); st2B = Buf()
        ot = [sb("ot%d" % i, [128, D], F32) for i in range(2)]; otB = [Buf() for _ in range(2)]
        for tb in range(NTB):
            za, zaB = pj()
            zb, zbB = pj()
            for half, (z, zB) in enumerate(((za, zaB), (zb, zbB))):
                for kc in range(8):
                    P.op("pe", lambda e, z=z, kc=kc, tb=tb, half=half: e.matmul(z[:], mT[:, kc, tb * 128:(tb + 1) * 128], woutb[:, kc, half * 512:(half + 1) * 512], start=(kc == 0), stop=(kc == 7)),
                         reads=[mTB, woutB], writes=[zB])
            c = 4 * tb
            P.op("act", lambda e, za=za, c=c: e.activation(out=T["junk"][:, 0:512], in_=za[:], func=AF.Square, accum_out=st2[:, c:c + 1]), reads=[zaB], writes=[B["junk"], st2B])
            P.op("act", lambda e, zb=zb, c=c: e.activation(out=T["junk"][:, 512:1024], in_=zb[:], func=AF.Square, accum_out=st2[:, c + 1:c + 2]), reads=[zbB], writes=[B["junk"], st2B])
            P.op("dve", lambda e, c=c: e.tensor_tensor(out=st2[:, c + 2:c + 3], in0=st2[:, c:c + 1], in1=st2[:, c + 1:c + 2], op=ALU.add), reads=[st2B], writes=[st2B])
            P.op("act", lambda e, c=c: e.activation(out=st2[:, c + 3:c + 4], in_=st2[:, c + 2:c + 3], func=AF.Ln, bias=EPS, scale=1.0 / D), reads=[st2B], writes=[st2B])
            P.op("act", lambda e, c=c: e.activation(out=st2[:, c + 2:c + 3], in_=st2[:, c + 3:c + 4], func=AF.Exp, scale=-0.5), reads=[st2B], writes=[st2B])
            oi = tb % 2
            for half, (z, zB) in enumerate(((za, zaB), (zb, zbB))):
                hs = slice(half * 512, (half + 1) * 512)
                P.op("dve", lambda e, z=z, c=c, hs=hs, oi=oi: e.scalar_tensor_tensor(out=ot[oi][:, hs], in0=z[:], scalar=st2[:, c + 2:c + 3], in1=gpost[:, hs], op0=ALU.mult, op1=ALU.mult),
                     reads=[zB, st2B, B["const"]], writes=[otB[oi]])
            P.dma(lambda e, tb=tb, oi=oi: e.dma_start(out=T["xt"][oi][:], in_=xo[(tb + 1) * 128:(tb + 2) * 128, :]), writes=[B["xt"][oi]])
            P.op("pool", lambda e, oi=oi, tb=tb: e.tensor_tensor(out=ot[oi][:], in0=ot[oi][:], in1=T["xt"][oi][:], op=ALU.add), reads=[otB[oi], B["xt"][oi]], writes=[otB[oi]])
            P.dma(lambda e, tb=tb, oi=oi: e.dma_start(out=xn_d[tb * 128:(tb + 1) * 128, :], in_=ot[oi][:]), reads=[otB[oi]], is_out=True)
        P.emit()
        ph.close()
    return nc


PAIRS = [[0, 1], [2, 3], [4, 5], [6, 7]]
NL = 2


def build_fused(nlayers=NL, dbg=False):
    nc = bass.Bass("TRN2", target_bir_lowering=False)
    dt_in = lambda n, s: nc.dram_tensor(n, s, F32, kind="ExternalInput").ap()
    xo = dt_in("xo", [TOK, D])
    gpre_d = dt_in("gpre_d", [NL, 128, 8])
    wq_d = dt_in("wq", [NL, D, 1024])
    cst_d = dt_in("cst_d", [128, 640])
    pm_d = dt_in("pm_d", [128, 12 * 128])
    flg_d = dt_in("flg_d", [128, 4])
    wpool_d = dt_in("wpool", [NL, D, 1024])
    poolw_d = dt_in("poolw_d", [NL, 128, 512])
    vec_d = dt_in("vec_d", [NL, 128, 32])
    wconv_d = dt_in("wconv", [NL, 4, D, 512])
    wmg_d = dt_in("wmg", [NL, 8, D, 384])
    wbr_d = dt_in("wbr", [NL, 8, 512, 384])
    wout_d = dt_in("wout", [NL, D, D])
    gpost_d = dt_in("gpost_d", [NL, 128, D])
    xn_d = nc.dram_tensor("xn", [TOK, D], F32, kind="ExternalOutput").ap()
    ib_h = [nc.dram_tensor("ib_h%d" % j, [1024, 512], BF16) for j in range(4)]
    ob_h = [nc.dram_tensor("ob_h%d" % j, [2 * 1024, 512], BF16) for j in range(4)]
    ib_y = nc.dram_tensor("ib_y", [256, S], BF16)
    ob_y = nc.dram_tensor("ob_y", [2 * 256, S], BF16)
    ibhB = [Buf() for _ in range(4)]; obhB = [Buf() for _ in range(4)]; ibyB = Buf(); obyB = Buf()
    P = Prog(nc)
    NT1 = TOK + 128
    with ExitStack() as st_:
        sb, ps = _alloc(nc, st_)
        T, B = {}, {}
        xres = sb("xres", [128, NTB, D], F32); xresB = [Buf() for _ in range(NTB)]
        B["junk"] = Buf()
        st = sb("st", [128, 3 * 16 * NL], F32); stB = Buf()
        st2 = sb("st2", [128, 4 * NTB * NL], F32); st2B = Buf()
        cst = sb("cst", [128, 640], BF16); B["const"] = Buf()
        ident = cst[:, 0:128]; negtri = cst[:, 128:256]; negones = cst[:, 256:384]; cmask = cst[:, 384:512]; cmask2 = cst[:, 384:640].rearrange("p (h q) -> p h q", h=2)
        flg = sb("flg", [128, 4], F32)
        gpre = sb("gpre", [128, NL, 8], F32)
        vec = sb("vec", [128, NL, 32], F32)
        tpf = ps("tpf", [128, 512], F32); tpB = Buf()
        tp = tpf[:].bitcast(BF16).rearrange("p (a b) -> p a b", a=8)
        KBall = ps("KBall", [128, 7, 512], F32)
        KB = [KBall[:, i, :] for i in range(7)]; KBB = [Buf() for _ in range(7)]
        pjk = [0]

        def pj():
            i = pjk[0] % 6
            pjk[0] += 1
            return KB[i], KBB[i]

        P.dma(lambda e: e.dma_start(out=cst[:], in_=cst_d), writes=[B["const"]], q="pool")
        P.dma(lambda e: e.dma_start(out=flg[:], in_=flg_d), writes=[B["const"]])
        for l in range(NL):
            P.dma(lambda e, l=l: e.dma_start(out=gpre[:, l, :], in_=gpre_d[l]), writes=[B["const"]])
            P.dma(lambda e, l=l: e.dma_start(out=vec[:, l, :], in_=vec_d[l]), writes=[B["const"]])
        for tb in range(NTB):
            P.dma(lambda e, tb=tb: e.dma_start(out=xres[:, tb, :], in_=xo[tb * 128:(tb + 1) * 128, :]), writes=[xresB[tb]])

        for l in range(nlayers):
            phW = ExitStack()
            wqb = phW.enter_context(nc.sbuf_tensor("wqb_%d" % l, [128, 8, 1024], BF16)); wqB = Buf()
            P.dma(lambda e, l=l: e.dma_start(out=wqb[:], in_=wq_d[l].rearrange("(kc p) n -> p kc n", p=128)), writes=[wqB], q="pool")
            ph = ExitStack()
            sbp = lambda n, s_, d, ph=ph: ph.enter_context(nc.sbuf_tensor(n, s_, d))
            hTn = sbp("hTn%d" % l, [128, 8, TOK], BF16); hTnB = [Buf() for _ in range(4)]
            T["hb"] = [sbp("hb%d_%d" % (i, l), [128, D], BF16) for i in range(2)]
            B["hb"] = [Buf() for i in range(2)]
            for tb in range(NTB):
                c = 3 * (16 * l + tb)
                hb, hbB = T["hb"][tb % 2], B["hb"][tb % 2]
                P.op("act", lambda e, tb=tb, c=c: e.activation(out=KBall[:, 5:7, :].rearrange("p a b -> p (a b)"), in_=xres[:, tb, :], func=AF.Square, accum_out=st[:, c:c + 1]),
                     reads=[xresB[tb]], writes=[KBB[5], KBB[6], stB])
                P.op("act", lambda e, c=c: e.activation(out=st[:, c + 1:c + 2], in_=st[:, c:c + 1], func=AF.Ln, bias=EPS, scale=1.0 / D), reads=[stB], writes=[stB])
                P.op("act", lambda e, c=c: e.activation(out=st[:, c + 2:c + 3], in_=st[:, c + 1:c + 2], func=AF.Exp, scale=-0.5), reads=[stB], writes=[stB])
                P.op("dve", lambda e, tb=tb, c=c, hb=hb: e.tensor_scalar_mul(out=hb[:], in0=xres[:, tb, :], scalar1=st[:, c + 2:c + 3]), reads=[xresB[tb], stB], writes=[hbB])
                for dc in range(8):
                    P.op("pe", lambda e, dc=dc, hb=hb: e.transpose(tp[:, dc, :], hb[:, dc * 128:(dc + 1) * 128], ident), reads=[hbB, B["const"]], writes=[tpB])
                for dc in range(8):
                    P.op("dve", lambda e, dc=dc, tb=tb, l=l: e.tensor_scalar_mul(out=hTn[:, dc, tb * 128:(tb + 1) * 128], in0=tp[:, dc, :], scalar1=gpre[:, l, dc:dc + 1]),
                         reads=[tpB, B["const"]], writes=[hTnB[tb // 4]])
                if tb % 4 == 3:
                    qd = tb // 4
                    for dc in range(8):
                        P.dma(lambda e, dc=dc, qd=qd: e.dma_start(out=ib_h[qd].ap()[dc * 128:(dc + 1) * 128, :], in_=hTn[:, dc, qd * 512:(qd + 1) * 512]), reads=[hTnB[qd]], writes=[ibhB[qd]])
                    P.dma(lambda e, qd=qd: e.collective_compute("AllGather", ALU.bypass, replica_groups=PAIRS, ins=[ib_h[qd].ap().opt()], outs=[ob_h[qd].ap().opt()]),
                          reads=[ibhB[qd]], writes=[obhB[qd]], q="pool", inc=1)
            P.barrier(); ph.close()

            ph = ExitStack()
            sbp = lambda n, s_, d, ph=ph: ph.enter_context(nc.sbuf_tensor(n, s_, d))
            L = "_%d" % l
            hch = [sbp("hch%d" % i + L, [128, 8, 512], BF16) for i in range(2)]; hchB = [Buf() for _ in range(2)]
            qT = sbp("qT" + L, [128, 2, S], BF16); kT = sbp("kT" + L, [128, 2, S], BF16)
            sgT = sbp("sgT" + L, [128, 2, S], BF16); vt = sbp("vt" + L, [128, 32, 256], BF16)
            qB = [Buf() for _ in range(8)]; kB = [Buf() for _ in range(8)]; sgB = [Buf() for _ in range(8)]; vB = [Buf() for _ in range(8)]
            ysb = sbp("ysbT" + L, [128, 2, S], BF16); ysbB = Buf()
            ee = [sbp("ee%d" % i + L, [128, 2, 512], F32) for i in range(2)]; eeB = [Buf() for _ in range(2)]
            spp = [sbp("spp%d" % i + L, [128, 2, 512], BF16) for i in range(3)]; sppB = [Buf() for _ in range(3)]
            AT = [sbp("AT%d" % i + L, [128, 2, 512], BF16) for i in range(2)]; ATB = [Buf() for _ in range(2)]
            Sf = sbp("Sf" + L, [128, 2, 512], F32); SfB = Buf()
            Sb = [sbp("Sb%d" % i + L, [128, 2, 512], BF16) for i in range(2)]; SbB = [Buf() for _ in range(2)]
            Z1p = [KBall[:, 0:2, :], KBall[:, 2:4, :]]; Z1pB = [Buf(), Buf()]
            Z2p = KBall[:, 4:6, :]; Z2pB = Buf()
            OOp = KB[6]; OOpB = KBB[6]
            Z1, Z1B = [KB[0], KB[2]], Z1pB
            def a2_groups(tc):
                hc, hcB = hch[tc % 2], hchB[tc % 2]
                sl = slice(tc * 512, (tc + 1) * 512)
                gl = []
                for cc in range(6):
                    for half in range(4):
                        def g(cc=cc, half=half):
                            for kc in range(2 * half, 2 * half + 2):
                                P.op("pe", lambda e, kc=kc: e.matmul(tpf[:], wqb[:, kc, cc * 128:(cc + 1) * 128], hc[:, kc, :], start=(kc == 0), stop=(kc == 7)),
                                     reads=[wqB, hcB], writes=[tpB])
                            if half == 3:
                                if cc < 2:
                                    P.op("dve", lambda e: e.tensor_scalar_mul(out=qT[:, cc, sl], in0=tpf[:], scalar1=0.125), reads=[tpB], writes=[qB[tc]])
                                elif cc < 4:
                                    P.op("dve", lambda e: e.tensor_copy(out=kT[:, cc - 2, sl], in_=tpf[:]), reads=[tpB], writes=[kB[tc]])
                                else:
                                    P.op("act", lambda e: e.activation(out=sgT[:, cc - 4, sl], in_=tpf[:], func=AF.Silu), reads=[tpB], writes=[sgB[tc]])
                        gl.append(g)
                for jb in range(4):
                    for half in range(2):
                        def g(jb=jb, half=half):
                            tb = 4 * tc + jb
                            for kc in range(4 * half, 4 * half + 4):
                                P.op("pe", lambda e, kc=kc: e.matmul(tpf[:, 0:256], hc[:, kc, jb * 128:(jb + 1) * 128], wqb[:, kc, 768:1024], start=(kc == 0), stop=(kc == 7)),
                                     reads=[wqB, hcB], writes=[tpB])
                            if half == 1:
                                P.op("dve", lambda e: e.tensor_copy(out=vt[:, tb, :], in_=tpf[:, 0:256]), reads=[tpB], writes=[vB[tc]])
                        gl.append(g)
                return gl

            def load_hch(tc):
                r, qd = tc // 4, tc % 4
                P.dma(lambda e: e.dma_start(out=hch[tc % 2][:, :, :], in_=ob_h[qd].ap()[r * 1024:(r + 1) * 1024, :].rearrange("(k p) t -> p k t", p=128)),
                      reads=[obhB[qd]], writes=[hchB[tc % 2]])
            load_hch(0)
            load_hch(1)
            for g in a2_groups(0):
                g()
            prs = []
            for G in range(8):
                for hp in range(2):
                    for kb in range(4 * G + 3, -1, -1):
                        prs.append((hp, G, kb))
            nS = {}
            hsl = [slice(0, 64), slice(64, 128)]

            def s1(t, pr):
                hp, G, kb = pr
                r = kb - 4 * G if kb >= 4 * G else 0
                c0 = 128 * r
                zi, ei, si = t % 2, t % 2, t % 3
                for hh in range(2):
                    P.op("pe", lambda e, hh=hh: e.matmul(Z1p[zi][:, hh, c0:512], kT[hsl[hh], hp, kb * 128:(kb + 1) * 128], qT[hsl[hh], hp, G * 512 + c0:(G + 1) * 512], start=True, stop=True),
                         reads=[kB[kb // 4], qB[G]], writes=[Z1pB[zi]])
                P.op("act", lambda e: e.activation(out=ee[ei][:, :, c0:512], in_=Z1p[zi][:, :, c0:512], func=AF.Exp), reads=[Z1pB[zi]], writes=[eeB[ei]])
                P.op("act", lambda e: e.activation(out=spp[si][:, :, c0:512], in_=ee[ei][:, :, c0:512], func=AF.Ln, bias=1.0, scale=1.0), reads=[eeB[ei]], writes=[sppB[si]])
                if kb >= 4 * G:
                    P.op("pool", lambda e: e.tensor_tensor(out=spp[si][:, :, c0:c0 + 128], in0=spp[si][:, :, c0:c0 + 128], in1=cmask2, op=ALU.mult),
                         reads=[sppB[si], B["const"]], writes=[sppB[si]])

            def s2(t, pr):
                hp, G, kb = pr
                diag = kb >= 4 * G
                r = kb - 4 * G if diag else 0
                c0 = 128 * r
                si, ai = t % 3, t % 2
                top = (kb == 4 * G + 3)
                n = nS.get((hp, G), 0)
                nS[(hp, G)] = n + 1
                if top:
                    P.op("dve", lambda e: e.memset(Sf[:], 0.0), writes=[SfB])
                for hh in range(2):
                    P.op("pe", lambda e, hh=hh: e.matmul(Z2p[:, hh, c0:512], kT[hsl[hh], hp, kb * 128:(kb + 1) * 128], qT[hsl[hh], hp, G * 512 + c0:(G + 1) * 512], start=True, stop=False),
                         reads=[kB[kb // 4], qB[G]], writes=[Z2pB])
                for hh in range(2):
                    P.op("pe", lambda e, hh=hh: e.matmul(Z2p[:, hh, c0:512], negtri, spp[si][:, hh, c0:512], start=False, stop=top),
                         reads=[sppB[si], B["const"]], writes=[Z2pB])
                if not top:
                    pjx = (n - 1) % 2
                    for hh in range(2):
                        P.op("pe", lambda e, hh=hh: e.matmul(Z2p[:, hh, c0:512], negones, Sb[pjx][:, hh, c0:512], start=False, stop=True),
                             reads=[SbB[pjx], B["const"]], writes=[Z2pB])
                if kb > 0:
                    P.op("dve", lambda e: e.tensor_tensor(out=Sb[n % 2][:, :, c0:512], in0=Sf[:, :, c0:512], in1=spp[si][:, :, c0:512], op=ALU.add),
                         reads=[SfB, sppB[si]], writes=[SbB[n % 2]])
                    if c0 > 0:
                        P.op("dve", lambda e: e.memset(Sb[n % 2][:, :, 0:c0], 0.0), writes=[SbB[n % 2]])
                    P.op("dve", lambda e: e.tensor_tensor(out=Sf[:, :, c0:512], in0=Sf[:, :, c0:512], in1=spp[si][:, :, c0:512], op=ALU.add),
                         reads=[SfB, sppB[si]], writes=[SfB])
                P.op("act", lambda e: e.activation(out=AT[ai][:, :, c0:512], in_=Z2p[:, :, c0:512], func=AF.Exp), reads=[Z2pB], writes=[ATB[ai]])
                if diag:
                    P.op("pool", lambda e: e.tensor_tensor(out=AT[ai][:, :, c0:c0 + 128], in0=AT[ai][:, :, c0:c0 + 128], in1=cmask2, op=ALU.mult),
                         reads=[ATB[ai], B["const"]], writes=[ATB[ai]])
                if top:
                    P.op("pool", lambda e: e.memset(AT[ai][:, :, 0:c0], 0.0), writes=[ATB[ai]])

            def s3(t, pr):
                hp, G, kb = pr
                diag = kb >= 4 * G
                c0 = 128 * (kb - 4 * G) if diag else 0
                ai = t % 2
                top = (kb == 4 * G + 3)
                last = (kb == 0)
                o0 = 0 if top else c0
                for hh in range(2):
                    P.op("pe", lambda e, hh=hh: e.matmul(OOp[hsl[hh], o0:512], vt[:, kb, hp * 128 + hh * 64:hp * 128 + (hh + 1) * 64], AT[ai][:, hh, o0:512],
                                                         start=top, stop=last, tile_position=(0, 64 * hh)),
                         reads=[vB[kb // 4], ATB[ai]], writes=[OOpB])
                if last:
                    gs = slice(G * 512, (G + 1) * 512)
                    P.op("dve", lambda e: e.tensor_tensor(out=ysb[:, hp, gs], in0=OOp[:, :], in1=sgT[:, hp, gs], op=ALU.mult),
                         reads=[OOpB, sgB[G]], writes=[ysbB])

            sched = {}
            t_base = 0
            for G in range(8):
                nst = 2 * (4 * G + 4)
                if G + 1 < 8:
                    gl = a2_groups(G + 1)
                    for j, g in enumerate(gl):
                        st_i = t_base + (j * (nst - 1)) // len(gl)
                        sched.setdefault(st_i, []).append(g)
                    if G + 2 < 8:
                        sched.setdefault(t_base, []).append(lambda G=G: load_hch(G + 2))
                t_base += nst
            s1(0, prs[0])
            for t in range(len(prs)):
                for g in sched.get(t, []):
                    g()
                if t + 1 < len(prs):
                    s1(t + 1, prs[t + 1])
                s2(t, prs[t])
                if t > 0:
                    s3(t - 1, prs[t - 1])
            s3(len(prs) - 1, prs[-1])
            for hp in range(2):
                P.dma(lambda e, hp=hp: e.dma_start(out=ib_y.ap()[hp * 128:(hp + 1) * 128, :], in_=ysb[:, hp, :]), reads=[ysbB], writes=[ibyB])
            P.barrier(); ph.close(); phW.close()

            phB = ExitStack()
            sbB_ = lambda n, s_, d, phB=phB: phB.enter_context(nc.sbuf_tensor(n, s_, d))
            hT = sbB_("hT" + L, [128, 8, NT1], BF16); hTB = Buf()
            ypT = sbB_("ypT" + L, [128, 4, TOK], BF16); ypB = Buf()
            ycT = sbB_("ycT" + L, [128, 4, TOK], BF16); ycB = Buf()
            for qd in range(4):
                P.dma(lambda e, qd=qd: e.dma_start(out=hT[:, :, 128 + qd * 512:128 + (qd + 1) * 512], in_=ib_h[qd].ap().rearrange("(k p) t -> p k t", p=128)), reads=[ibhB[qd]], writes=[hTB])
            ph = ExitStack()
            sbp = lambda n, s_, d, ph=ph: ph.enter_context(nc.sbuf_tensor(n, s_, d))
            hal = sbp("hal" + L, [128, 8, 128], BF16); halB = Buf()
            P.dma(lambda e: e.dma_start(out=hal[:, :, :], in_=ob_h[3].ap()[0:1024, 384:512].rearrange("(k p) t -> p k t", p=128)), reads=[obhB[3]], writes=[halB])
            P.op("dve", lambda e: e.tensor_scalar_mul(out=hT[:, :, 0:128], in0=hal[:, :, :], scalar1=flg[:, 0:1]), reads=[halB, B["const"]], writes=[hTB])
            wch = [sbp("wch%d" % i + L, [128, 8, 512], BF16) for i in range(2)]; wchB = [Buf() for _ in range(2)]
            pvt = sbp("pvt" + L, [128, 17, 512], BF16); pvtB = Buf()
            spg = sbp("spg" + L, [128, 4, TOK], BF16); spgB = Buf()
            plb = [sbp("plb%d" % i + L, [128, 512], BF16) for i in range(2)]; plbB = [Buf() for _ in range(2)]
            poolw = sbp("poolw" + L, [128, 512], BF16); poolwB = Buf()
            pm = sbp("pm" + L, [128, 12 * 128], BF16)
            P.dma(lambda e: e.dma_start(out=pm[:], in_=pm_d), writes=[poolwB], q="pool")
            P.dma(lambda e, l=l: e.dma_start(out=poolw[:], in_=poolw_d[l]), writes=[poolwB], q="pool")
            P.dma(lambda e, l=l: e.dma_start(out=wch[0][:, :, :], in_=wpool_d[l][:, 0:512].rearrange("(kc p) n -> p kc n", p=128)), writes=[wchB[0]], q="pool")
            P.dma(lambda e, l=l: e.dma_start(out=wch[1][:, :, :], in_=wpool_d[l][:, 512:1024].rearrange("(kc p) n -> p kc n", p=128)), writes=[wchB[1]], q="pool")
            for blk in range(17):
                z, zB = pj()
                for kc in range(8):
                    P.op("pe", lambda e, z=z, kc=kc, blk=blk: e.matmul(z[:], hT[:, kc, blk * 128:(blk + 1) * 128], wch[0][:, kc, :], start=(kc == 0), stop=(kc == 7)),
                         reads=[hTB, wchB[0]], writes=[zB])
                P.op("dve", lambda e, z=z, blk=blk: e.tensor_copy(out=pvt[:, blk, :], in_=z[:]), reads=[zB], writes=[pvtB])
            for cc in range(4):
                for tc in range(4):
                    z, zB = pj()
                    for kc in range(8):
                        P.op("pe", lambda e, z=z, kc=kc, cc=cc, tc=tc: e.matmul(z[:], wch[1][:, kc, cc * 128:(cc + 1) * 128], hT[:, kc, 128 + tc * 512:128 + (tc + 1) * 512], start=(kc == 0), stop=(kc == 7)),
                             reads=[hTB, wchB[1]], writes=[zB])
                    P.op("act", lambda e, z=z, cc=cc, tc=tc: e.activation(out=spg[:, cc, tc * 512:(tc + 1) * 512], in_=z[:], func=AF.Silu), reads=[zB], writes=[spgB])
            kk = 0
            for g in range(4):
                for tc in range(4):
                    z, zB = pj()
                    for jx in range(4):
                        b = 1 + 4 * tc + jx
                        first = (b == 1)
                        pmc = pm[:, (3 * g + (2 if first else 0)) * 128:(3 * g + (2 if first else 0) + 1) * 128]
                        pmp = pm[:, (3 * g + 1) * 128:(3 * g + 2) * 128]
                        P.op("pe", lambda e, z=z, jx=jx, b=b, g=g, pmc=pmc: e.matmul(z[:, jx * 128:(jx + 1) * 128], pvt[:, b, g * 128:(g + 1) * 128], pmc, start=True, stop=False),
                             reads=[pvtB, poolwB], writes=[zB])
                        P.op("pe", lambda e, z=z, jx=jx, b=b, g=g, pmp=pmp: e.matmul(z[:, jx * 128:(jx + 1) * 128], pvt[:, b - 1, g * 128:(g + 1) * 128], pmp, start=False, stop=True),
                             reads=[pvtB, poolwB], writes=[zB])
                    pi = kk % 2; kk += 1
                    P.op("act", lambda e, z=z, pi=pi: e.activation(out=plb[pi][:], in_=z[:], func=AF.Copy), reads=[zB], writes=[plbB[pi]])
                    z2, z2B = pj()
                    P.op("pe", lambda e, z2=z2, pi=pi, g=g: e.matmul(z2[:], poolw[:, g * 128:(g + 1) * 128], plb[pi][:], start=True, stop=True),
                         reads=[plbB[pi], poolwB], writes=[z2B])
                    sl = slice(tc * 512, (tc + 1) * 512)
                    P.op("dve", lambda e, z2=z2, g=g, sl=sl, l=l: e.scalar_tensor_tensor(out=ypT[:, g, sl], in0=z2[:], scalar=vec[:, l, g:g + 1], in1=spg[:, g, sl], op0=ALU.mult, op1=ALU.mult),
                         reads=[z2B, spgB, B["const"]], writes=[ypB])
            P.barrier(); ph.close()
            ph = ExitStack()
            sbp = lambda n, s_, d, ph=ph: ph.enter_context(nc.sbuf_tensor(n, s_, d))
            wch = [sbp("wcv%d" % i + L, [128, 8, 512], BF16) for i in range(2)]; wchB = [Buf() for _ in range(2)]
            cxT = sbp("cxT" + L, [128, NT1], F32); gcT = sbp("gcT" + L, [128, NT1], F32)
            gbT = sbp("gbT" + L, [128, TOK], F32); scg = sbp("scg" + L, [128, TOK], F32)
            zz = sbp("zz" + L, [128, NT1], F32); acc = sbp("acc" + L, [128, TOK], F32)
            cxB, gcB, gbB, scgB, zzB, accB = Buf(), Buf(), Buf(), Buf(), Buf(), Buf()

            def load_conv(u):
                P.dma(lambda e, u=u, l=l: e.dma_start(out=wch[u % 2][:, :, :], in_=wconv_d[l, u].rearrange("(kc p) n -> p kc n", p=128)), writes=[wchB[u % 2]], q="pool")
            load_conv(0)
            for u in range(4):
                if u + 1 < 4:
                    load_conv(u + 1)
                if u == 2:
                    P.dma(lambda e: e.collective_compute("AllGather", ALU.bypass, replica_groups=PAIRS, ins=[ib_y.ap().opt()], outs=[ob_y.ap().opt()]),
                          reads=[ibyB], writes=[obyB], q="pool", inc=1)
                w_, wB_ = wch[u % 2], wchB[u % 2]
                for which, dst, dB, c_off in ((0, cxT, cxB, 0), (2, gcT, gcB, 256)):
                    for tcx in range(5):
                        t0, t1 = (0, 128) if tcx == 0 else (128 + (tcx - 1) * 512, 128 + tcx * 512)
                        z, zB = pj()
                        for kc in range(8):
                            P.op("pe", lambda e, z=z, kc=kc, t0=t0, t1=t1, c_off=c_off, w_=w_: e.matmul(z[:, 0:t1 - t0], w_[:, kc, c_off:c_off + 128], hT[:, kc, t0:t1], start=(kc == 0), stop=(kc == 7)),
                                 reads=[hTB, wB_], writes=[zB])
                        P.op("act", lambda e, z=z, t0=t0, t1=t1, dst=dst: e.activation(out=dst[:, t0:t1], in_=z[:, 0:t1 - t0], func=AF.Copy), reads=[zB], writes=[dB])
                for c_off, dst, dB, fn in ((128, gbT, gbB, AF.Copy), (384, scg, scgB, AF.Silu)):
                    for tc in range(4):
                        z, zB = pj()
                        for kc in range(8):
                            P.op("pe", lambda e, z=z, kc=kc, tc=tc, c_off=c_off, w_=w_: e.matmul(z[:], w_[:, kc, c_off:c_off + 128], hT[:, kc, 128 + tc * 512:128 + (tc + 1) * 512], start=(kc == 0), stop=(kc == 7)),
                                 reads=[hTB, wB_], writes=[zB])
                        P.op("act", lambda e, z=z, tc=tc, dst=dst, fn=fn: e.activation(out=dst[:, tc * 512:(tc + 1) * 512], in_=z[:], func=fn), reads=[zB], writes=[dB])
                w0 = vec[:, l, 4 + u:5 + u]; w1 = vec[:, l, 8 + u:9 + u]; w2 = vec[:, l, 12 + u:13 + u]; cb = vec[:, l, 16 + u:17 + u]
                P.op("dve", lambda e: e.tensor_tensor(out=zz[:], in0=gcT[:], in1=cxT[:], op=ALU.mult), reads=[gcB, cxB], writes=[zzB])
                P.op("dve", lambda e, w2=w2, cb=cb: e.tensor_scalar(out=acc[:], in0=zz[:, 128:NT1], scalar1=w2, scalar2=cb, op0=ALU.mult, op1=ALU.add), reads=[zzB, B["const"]], writes=[accB])
                P.op("dve", lambda e, w1=w1: e.scalar_tensor_tensor(out=acc[:], in0=zz[:, 127:NT1 - 1], scalar=w1, in1=acc[:], op0=ALU.mult, op1=ALU.add), reads=[zzB, accB, B["const"]], writes=[accB])
                P.op("dve", lambda e, w0=w0: e.scalar_tensor_tensor(out=acc[:], in0=zz[:, 126:NT1 - 2], scalar=w0, in1=acc[:], op0=ALU.mult, op1=ALU.add), reads=[zzB, accB, B["const"]], writes=[accB])
                P.op("dve", lambda e: e.tensor_tensor(out=acc[:], in0=acc[:], in1=gbT[:], op=ALU.mult), reads=[accB, gbB], writes=[accB])
                P.op("dve", lambda e, u=u: e.tensor_tensor(out=ycT[:, u, :], in0=acc[:], in1=scg[:], op=ALU.mult), reads=[accB, scgB], writes=[ycB])
            P.barrier(); ph.close()
            ysT = sbB_("ysT" + L, [128, 4, TOK], BF16); ysB = Buf()
            ph = ExitStack()
            sbp = lambda n, s_, d, ph=ph: ph.enter_context(nc.sbuf_tensor(n, s_, d))
            ta = [sbp("ta%d" % i + L, [128, TOK], BF16) for i in range(2)]; taB = [Buf() for _ in range(2)]
            tb_ = [sbp("tb%d" % i + L, [128, TOK], BF16) for i in range(2)]; tbB = [Buf() for _ in range(2)]
            for i in range(4):
                r, hp = i // 2, i % 2
                row0 = r * 256 + hp * 128
                P.dma(lambda e, i=i, row0=row0: e.dma_start(out=ta[i % 2][:], in_=ob_y.ap()[row0:row0 + 128, 0:TOK]), reads=[obyB], writes=[taB[i % 2]])
                P.dma(lambda e, i=i, row0=row0: e.dma_start(out=tb_[i % 2][:], in_=ob_y.ap()[row0:row0 + 128, TOK:S]), reads=[obyB], writes=[tbB[i % 2]])
                P.op("dve", lambda e, i=i: e.tensor_scalar_mul(out=ta[i % 2][:], in0=ta[i % 2][:], scalar1=flg[:, 1:2]), reads=[taB[i % 2], B["const"]], writes=[taB[i % 2]])
                P.op("dve", lambda e, i=i: e.scalar_tensor_tensor(out=ysT[:, i, :], in0=tb_[i % 2][:], scalar=flg[:, 2:3], in1=ta[i % 2][:], op0=ALU.mult, op1=ALU.add),
                     reads=[taB[i % 2], tbB[i % 2], B["const"]], writes=[ysB])
            P.barrier(); ph.close()
            ph6 = ExitStack()
            mT = ph6.enter_context(nc.sbuf_tensor("mT" + L, [128, 8, TOK], BF16)); mTB = Buf()
            ph = ExitStack()
            sbp = lambda n, s_, d, ph=ph: ph.enter_context(nc.sbuf_tensor(n, s_, d))
            wmg = [sbp("wmg%d" % i + L, [128, 8, 384], BF16) for i in range(2)]; wmgB = [Buf() for _ in range(2)]
            wbr = [sbp("wbr%d" % i + L, [128, 4, 384], BF16) for i in range(1)] * 2; wbrB = [Buf()] * 2
            gsb = [sbp("gsb%d" % i + L, [128, 512], F32) for i in range(1)] * 2; gsbB = [Buf()] * 2
            mac = [sbp("mac%d" % i + L, [128, 512], F32) for i in range(1)] * 2; macB = [Buf()] * 2
            tmpb = [sbp("tmpb%d" % i + L, [128, 512], F32) for i in range(1)] * 2; tmpB = [Buf()] * 2

            def load_mg(dcn):
                P.dma(lambda e, dcn=dcn, l=l: e.dma_start(out=wmg[dcn % 2][:, :, :], in_=wmg_d[l, dcn].rearrange("(kc p) n -> p kc n", p=128)), writes=[wmgB[dcn % 2]], q="pool")

            def load_br(dcn):
                P.dma(lambda e, dcn=dcn, l=l: e.dma_start(out=wbr[dcn % 2][:, :, :], in_=wbr_d[l, dcn].rearrange("(kc p) n -> p kc n", p=128)), writes=[wbrB[dcn % 2]], q="pool")
            yTs = [(ypT, ypB), (ycT, ycB), (ysT, ysB)]
            gk = 0
            load_mg(0)
            for dcn in range(8):
                load_br(dcn)
                if dcn + 1 < 8:
                    load_mg(dcn + 1)
                wm_, wmB_ = wmg[dcn % 2], wmgB[dcn % 2]
                wb_, wbB_ = wbr[dcn % 2], wbrB[dcn % 2]
                for tc in range(4):
                    sl = slice(tc * 512, (tc + 1) * 512)
                    mi = (dcn * 4 + tc) % 2
                    for n in range(3):
                        zg, zgB = pj()
                        for kc in range(8):
                            P.op("pe", lambda e, zg=zg, kc=kc, n=n, tc=tc, wm_=wm_: e.matmul(zg[:], wm_[:, kc, n * 128:(n + 1) * 128], hT[:, kc, 128 + tc * 512:128 + (tc + 1) * 512], start=(kc == 0), stop=(kc == 7)),
                                 reads=[hTB, wmB_], writes=[zgB])
                        gi = gk % 2; gk += 1
                        P.op("act", lambda e, zg=zg, gi=gi: e.activation(out=gsb[gi][:], in_=zg[:], func=AF.Sigmoid), reads=[zgB], writes=[gsbB[gi]])
                        zp, zpB = pj()
                        yT_, yB_ = yTs[n]
                        for wc in range(4):
                            P.op("pe", lambda e, zp=zp, wc=wc, n=n, sl=sl, wb_=wb_, yT_=yT_: e.matmul(zp[:], wb_[:, wc, n * 128:(n + 1) * 128], yT_[:, wc, sl], start=(wc == 0), stop=(wc == 3)),
                                 reads=[yB_, wbB_], writes=[zpB])
                        if n == 0:
                            P.op("dve", lambda e, zp=zp, gi=gi, mi=mi: e.tensor_tensor(out=mac[mi][:], in0=zp[:], in1=gsb[gi][:], op=ALU.mult), reads=[zpB, gsbB[gi]], writes=[macB[mi]])
                        else:
                            P.op("dve", lambda e, zp=zp, gi=gi, mi=mi: e.tensor_tensor(out=tmpb[mi][:], in0=zp[:], in1=gsb[gi][:], op=ALU.mult), reads=[zpB, gsbB[gi]], writes=[tmpB[mi]])
                            if n == 1:
                                P.op("pool", lambda e, mi=mi: e.tensor_tensor(out=mac[mi][:], in0=mac[mi][:], in1=tmpb[mi][:], op=ALU.add), reads=[macB[mi], tmpB[mi]], writes=[macB[mi]])
                            else:
                                P.op("pool", lambda e, mi=mi, dcn=dcn, sl=sl: e.tensor_tensor(out=mT[:, dcn, sl], in0=mac[mi][:], in1=tmpb[mi][:], op=ALU.add), reads=[macB[mi], tmpB[mi]], writes=[mTB])
            if dbg and l == 0:
                dbg_d = nc.dram_tensor("dbg", [5, 128, NT1], F32, kind="ExternalOutput").ap()
                P.dma(lambda e: e.dma_start(out=dbg_d[0], in_=hT[:, 0, :]), reads=[hTB], is_out=True, q="pool")
                P.dma(lambda e: e.dma_start(out=dbg_d[1][:, 0:TOK], in_=ypT[:, 0, :]), reads=[ypB], is_out=True, q="pool")
                P.dma(lambda e: e.dma_start(out=dbg_d[2][:, 0:TOK], in_=ycT[:, 0, :]), reads=[ycB], is_out=True, q="pool")
                P.dma(lambda e: e.dma_start(out=dbg_d[3][:, 0:TOK], in_=ysT[:, 0, :]), reads=[ysB], is_out=True, q="pool")
                P.dma(lambda e: e.dma_start(out=dbg_d[4][:, 0:TOK], in_=mT[:, 0, :]), reads=[mTB], is_out=True, q="pool")
            P.barrier(); ph.close()
            ph = ExitStack()
            sbp = lambda n, s_, d, ph=ph: ph.enter_context(nc.sbuf_tensor(n, s_, d))
            woutb = sbp("woutb" + L, [128, 8, D], BF16); woutB = Buf()
            gpost = sbp("gpost" + L, [128, D], F32); gpostB = Buf()
            ot = [sbp("ot%d" % i + L, [128, 512], F32) for i in range(2)]; otB = [Buf() for _ in range(2)]
            P.dma(lambda e, l=l: e.dma_start(out=woutb[:, :, :], in_=wout_d[l].rearrange("(kc p) n -> p kc n", p=128)), writes=[woutB], q="pool")
            P.dma(lambda e, l=l: e.dma_start(out=gpost[:], in_=gpost_d[l]), writes=[gpostB])
            for tb in range(NTB):
                za, zaB = pj()
                zb, zbB = pj()
                for half, (z, zB) in enumerate(((za, zaB), (zb, zbB))):
                    for kc in range(8):
                        P.op("pe", lambda e, z=z, kc=kc, tb=tb, half=half: e.matmul(z[:], mT[:, kc, tb * 128:(tb + 1) * 128], woutb[:, kc, half * 512:(half + 1) * 512], start=(kc == 0), stop=(kc == 7)),
                             reads=[mTB, woutB], writes=[zB])
                c = 4 * (NTB * l + tb)
                P.op("act", lambda e, za=za, c=c: e.activation(out=KB[6], in_=za[:], func=AF.Square, accum_out=st2[:, c:c + 1]), reads=[zaB], writes=[KBB[6], st2B])
                P.op("act", lambda e, zb=zb, c=c: e.activation(out=KB[6], in_=zb[:], func=AF.Square, accum_out=st2[:, c + 1:c + 2]), reads=[zbB], writes=[KBB[6], st2B])
                P.op("dve", lambda e, c=c: e.tensor_tensor(out=st2[:, c + 2:c + 3], in0=st2[:, c:c + 1], in1=st2[:, c + 1:c + 2], op=ALU.add), reads=[st2B], writes=[st2B])
                P.op("act", lambda e, c=c: e.activation(out=st2[:, c + 3:c + 4], in_=st2[:, c + 2:c + 3], func=AF.Ln, bias=EPS, scale=1.0 / D), reads=[st2B], writes=[st2B])
                P.op("act", lambda e, c=c: e.activation(out=st2[:, c + 2:c + 3], in_=st2[:, c + 3:c + 4], func=AF.Exp, scale=-0.5), reads=[st2B], writes=[st2B])
                oi = tb % 2
                for half, (z, zB) in enumerate(((za, zaB), (zb, zbB))):
                    hs = slice(half * 512, (half + 1) * 512)
                    P.op("dve", lambda e, z=z, c=c, hs=hs, half=half: e.scalar_tensor_tensor(out=ot[half][:], in0=z[:], scalar=st2[:, c + 2:c + 3], in1=gpost[:, hs], op0=ALU.mult, op1=ALU.mult),
                         reads=[zB, st2B, gpostB], writes=[otB[half]])
                    P.op("pool", lambda e, half=half, tb=tb, hs=hs: e.tensor_tensor(out=xres[:, tb, hs], in0=ot[half][:], in1=xres[:, tb, hs], op=ALU.add), reads=[otB[half], xresB[tb]], writes=[xresB[tb]])
                if l == nlayers - 1:
                    P.dma(lambda e, tb=tb: e.dma_start(out=xn_d[tb * 128:(tb + 1) * 128, :], in_=xres[:, tb, :]), reads=[xresB[tb]], is_out=True)
            P.barrier(); ph.close(); ph6.close(); phB.close()
        P.emit()
    return nc


from concourse.bass_utils import run_bass_kernel_spmd

_PROGS = {}


def _consts_A():
    k = np.arange(128)
    ident = np.eye(128, dtype=np.float32)
    negtri = -(k[:, None] >= k[None, :]).astype(np.float32)
    negones = -np.ones((128, 128), np.float32)
    cmask = (k[:, None] < k[None, :]).astype(np.float32)
    return np.ascontiguousarray(np.concatenate([ident, negtri, negones, cmask, cmask], axis=1))


def _pool_mats(p):
    s = np.arange(128)[:, None]
    t = np.arange(128)[None, :]
    out = []
    for w in (2, 4, 8, 16):
        cur = ((s <= t) & (s > t - w)).astype(np.float32) / w - (s == t).astype(np.float32)
        prev = (s > t + 128 - w).astype(np.float32) / w
        if p == 0:
            cnt = np.minimum(t + 1, w).astype(np.float32)
            first = ((s <= t) & (s > t - w)).astype(np.float32) / cnt - (s == t).astype(np.float32)
        else:
            first = cur
        out += [cur, prev, first]
    return np.ascontiguousarray(np.concatenate(out, axis=1).astype(np.float32))


def _run_layer(x, l, pre_norm_g, w_in, pool_w, pool_scale, conv_w, conv_b, w_branch, w_out, post_norm_g):
    if "A" not in _PROGS:
        _PROGS["A"] = build_A()
        _PROGS["B"] = build_B()
    Wl = w_in[l]
    gpre = np.ascontiguousarray(pre_norm_g[l].reshape(8, 128).T)
    cstA = _consts_A()
    mapsA = []
    for c in range(8):
        b, p = c // 2, c % 2
        o = 256 * p
        wq = np.concatenate([Wl[:, 3072 + o:3072 + o + 256], Wl[:, 3584 + o:3584 + o + 256],
                             Wl[:, 4608 + o:4608 + o + 256], Wl[:, 4096 + o:4096 + o + 256]], axis=1)
        mapsA.append({"xf": np.ascontiguousarray(x[b]), "gpre_d": gpre, "wq": np.ascontiguousarray(wq), "cst_d": cstA})
    resA = run_bass_kernel_spmd(_PROGS["A"], mapsA, core_ids=list(range(8))).results
    ys_full = np.zeros((4, 512, S), np.float32)
    for c in range(8):
        b, p = c // 2, c % 2
        y = np.asarray(resA[c]["ysb"])
        for hp in range(2):
            ys_full[b, 256 * p + 128 * hp:256 * p + 128 * hp + 128, :] = y[hp]
    wpool = np.ascontiguousarray(Wl[:, 0:1024])
    poolw = np.ascontiguousarray(np.concatenate([pool_w[l][g] for g in range(4)], axis=1))
    vec = np.zeros((128, 32), np.float32)
    vec[:, 0:4] = pool_scale[l].reshape(4, 128).T
    for k in range(3):
        vec[:, 4 + 4 * k:8 + 4 * k] = conv_w[l][k].reshape(4, 128).T
    vec[:, 16:20] = conv_b[l].reshape(4, 128).T
    wconv = np.stack([np.concatenate([Wl[:, 1024 + 512 * j + 128 * u:1024 + 512 * j + 128 * u + 128] for j in range(4)], axis=1) for u in range(4)])
    wmg = np.stack([np.concatenate([Wl[:, 5120 + 1024 * n + 128 * d:5120 + 1024 * n + 128 * d + 128] for n in range(3)], axis=1) for d in range(8)])
    wbr = np.stack([np.concatenate([w_branch[l][n][:, 128 * d:128 * d + 128] for n in range(3)], axis=1) for d in range(8)])
    gpost = np.ascontiguousarray(np.tile(post_norm_g[l][None, :], (128, 1)))
    ident = np.eye(128, dtype=np.float32)
    mapsB = []
    for c in range(8):
        b, p = c // 2, c % 2
        xo = np.zeros((TOK + 128, D), np.float32)
        if p == 1:
            xo[:] = x[b, TOK - 128:S]
        else:
            xo[128:] = x[b, 0:TOK]
        ysbm = np.ascontiguousarray(ys_full[b][:, p * TOK:(p + 1) * TOK].reshape(4, 128, TOK))
        mapsB.append({"xo": xo, "gpre_d": gpre, "cst_d": ident, "pm_d": _pool_mats(p), "wpool": wpool,
                      "poolw_d": poolw, "vec_d": vec, "wconv": np.ascontiguousarray(wconv), "wmg": np.ascontiguousarray(wmg),
                      "wbr": np.ascontiguousarray(wbr), "wout": np.ascontiguousarray(w_out[l]), "gpost_d": gpost, "ysbm": ysbm})
    resB = run_bass_kernel_spmd(_PROGS["B"], mapsB, core_ids=list(range(8))).results
    xn = np.empty_like(x)
    for c in range(8):
        b, p = c // 2, c % 2
        xn[b, p * TOK:(p + 1) * TOK] = np.asarray(resB[c]["xn"])
    return xn, ys_full


def _fused_inputs(x, pre_norm_g, w_in, pool_w, pool_scale, conv_w, conv_b, w_branch, w_out, post_norm_g):
    gpre = np.stack([pre_norm_g[l].reshape(8, 128).T for l in range(NL)])
    wpool = np.stack([w_in[l][:, 0:1024] for l in range(NL)])
    poolw = np.stack([np.concatenate([pool_w[l][g] for g in range(4)], axis=1) for l in range(NL)])
    vec = np.zeros((NL, 128, 32), np.float32)
    for l in range(NL):
        vec[l, :, 0:4] = pool_scale[l].reshape(4, 128).T
        for k in range(3):
            vec[l, :, 4 + 4 * k:8 + 4 * k] = conv_w[l][k].reshape(4, 128).T
        vec[l, :, 16:20] = conv_b[l].reshape(4, 128).T
    wconv = np.stack([np.stack([np.concatenate([w_in[l][:, 1024 + 512 * j + 128 * u:1024 + 512 * j + 128 * u + 128] for j in range(4)], axis=1) for u in range(4)]) for l in range(NL)])
    wmg = np.stack([np.stack([np.concatenate([w_in[l][:, 5120 + 1024 * n + 128 * d:5120 + 1024 * n + 128 * d + 128] for n in range(3)], axis=1) for d in range(8)]) for l in range(NL)])
    wbr = np.stack([np.stack([np.concatenate([w_branch[l][n][:, 128 * d:128 * d + 128] for n in range(3)], axis=1) for d in range(8)]) for l in range(NL)])
    gpost = np.stack([np.tile(post_norm_g[l][None, :], (128, 1)) for l in range(NL)])
    shared = {"gpre_d": gpre, "cst_d": _consts_A(), "wpool": wpool, "poolw_d": poolw, "vec_d": vec, "wconv": wconv,
              "wmg": wmg, "wbr": wbr, "wout": np.asarray(w_out), "gpost_d": gpost}
    shared = {k: np.ascontiguousarray(v, dtype=np.float32) for k, v in shared.items()}
    maps = []
    for c in range(8):
        b, p = c // 2, c % 2
        o = 256 * p
        wq = np.stack([np.concatenate([w_in[l][:, 3072 + o:3072 + o + 256], w_in[l][:, 3584 + o:3584 + o + 256],
                                       w_in[l][:, 4608 + o:4608 + o + 256], w_in[l][:, 4096 + o:4096 + o + 256]], axis=1) for l in range(NL)])
        flg = np.zeros((128, 4), np.float32)
        flg[:, 0] = float(p == 1)
        flg[:, 1] = float(p == 0)
        flg[:, 2] = float(p == 1)
        m = dict(shared)
        m.update({"xo": np.ascontiguousarray(x[b, p * TOK:(p + 1) * TOK]), "wq": np.ascontiguousarray(wq),
                  "pm_d": _pool_mats(p), "flg_d": flg})
        maps.append(m)
    return maps


def kernel(x, pre_norm_g, w_in, pool_w, pool_scale, conv_w, conv_b, w_branch, w_out, post_norm_g):
    args = [np.asarray(a, dtype=np.float32) for a in (pre_norm_g, w_in, pool_w, pool_scale, conv_w, conv_b, w_branch, w_out, post_norm_g)]
    x = np.asarray(x, dtype=np.float32)
    if "F" not in _PROGS:
        _PROGS["F"] = build_fused()
    maps = _fused_inputs(x, *args)
    res = run_bass_kernel_spmd(_PROGS["F"], maps, core_ids=list(range(8))).results
    out = np.empty_like(x)
    for c in range(8):
        b, p = c // 2, c % 2
        out[b, p * TOK:(p + 1) * TOK] = np.asarray(res[c]["xn"])
    return out
```
